# Optimizing a Trainium2 kernel written in Bass

```python
import math
import jax, jax.numpy as jnp
from jax import lax
import numpy as np

D_MODEL = 1024
BATCH = 8
SEQ = 2048
DEPTH = 2
DEC_BATCH = 128
DEC_SEQ = 8
PAST_LEN = 16384
PAGE_SIZE = 128

N_MIXERS = 2
N_GLA = (DEPTH + 1) // 2
N_CMLP = DEPTH // 2
GLA_HEADS = 4
GLA_DK = D_MODEL // 2 // GLA_HEADS
GLA_DV = D_MODEL // GLA_HEADS
GLA_KW = GLA_HEADS * GLA_DK
GLA_VW = GLA_HEADS * GLA_DV
GLA_GATE_RANK = 16
GLA_GATE_TAU = 16.0
GLA_CHUNK = 64
GLA_IN_WIDTH = 2 * GLA_KW + 2 * GLA_VW + GLA_GATE_RANK
CMLP_WIDTH = D_MODEL
CMLP_GROUPS = 4
CMLP_GROUP_DIM = CMLP_WIDTH // CMLP_GROUPS
CMLP_CHUNK = 128
EPS = 1e-6

kernel_name = "hybrid_gla_chunkmlp_decode_step"


def rmsnorm(x, g):
    xf = x.astype(jnp.float32)
    y = xf * lax.rsqrt(jnp.mean(xf * xf, axis=-1, keepdims=True) + EPS)
    return (y * g.astype(jnp.float32)).astype(x.dtype)


def layernorm(x, g, b):
    xf = x.astype(jnp.float32)
    mu = jnp.mean(xf, axis=-1, keepdims=True)
    var = jnp.mean(jnp.square(xf - mu), axis=-1, keepdims=True)
    y = (xf - mu) * lax.rsqrt(var + EPS)
    return (y * g.astype(jnp.float32) + b.astype(jnp.float32)).astype(x.dtype)


def gla_chunked(q, k, v, log_a, s0, chunk):
    b, t, h, dk = q.shape
    dv = v.shape[-1]
    n = t // chunk

    def to_chunks(z):
        return z.astype(jnp.float32).reshape(b, n, chunk, h, z.shape[-1]).transpose(1, 0, 3, 2, 4)

    qc, kc, vc, ac = to_chunks(q), to_chunks(k), to_chunks(v), to_chunks(log_a)
    causal = jnp.tril(jnp.ones((chunk, chunk), dtype=bool))

    def step(state, inp):
        qi, ki, vi, ai = inp
        cum = jnp.cumsum(ai, axis=2)
        diff = cum[:, :, :, None, :] - cum[:, :, None, :, :]
        decay = jnp.exp(jnp.where(causal[:, :, None], diff, -jnp.inf))
        scores = jnp.einsum('bhtd,bhsd,bhtsd->bhts', qi, ki, decay)
        o_intra = jnp.einsum('bhts,bhsv->bhtv', scores, vi)
        o_inter = jnp.einsum('bhtd,bhdv->bhtv', qi * jnp.exp(cum), state)
        last = cum[:, :, -1:, :]
        k_dec = ki * jnp.exp(last - cum)
        new_state = jnp.exp(last[:, :, 0, :])[..., None] * state + jnp.einsum('bhsd,bhsv->bhdv', k_dec, vi)
        return new_state, o_intra + o_inter

    s_final, o = lax.scan(step, s0.astype(jnp.float32), (qc, kc, vc, ac))
    o = o.transpose(1, 0, 3, 2, 4).reshape(b, t, h, dv)
    return o, s_final


def gla_mixer(h, s0, w_in, w_a_up, b_a_up, g_onorm, w_out):
    bsz, t, _ = h.shape
    proj = h @ w_in
    q, k, v, gate, a_low = jnp.split(
        proj, [GLA_KW, 2 * GLA_KW, 2 * GLA_KW + GLA_VW, 2 * GLA_KW + 2 * GLA_VW], axis=-1)
    q = q.reshape(bsz, t, GLA_HEADS, GLA_DK) * (GLA_DK ** -0.5)
    k = k.reshape(bsz, t, GLA_HEADS, GLA_DK)
    v = v.reshape(bsz, t, GLA_HEADS, GLA_DV)
    gate_logit = (a_low @ w_a_up + b_a_up).astype(jnp.float32)
    log_a = (jax.nn.log_sigmoid(gate_logit) / GLA_GATE_TAU).reshape(bsz, t, GLA_HEADS, GLA_DK)
    o, s_new = gla_chunked(q, k, v, log_a, s0, math.gcd(t, GLA_CHUNK))
    o = rmsnorm(o, g_onorm).astype(h.dtype)
    o = o.reshape(bsz, t, GLA_VW) * jax.nn.silu(gate)
    return o @ w_out, s_new


def cmlp_mixer(h, w_in, ln_g, ln_b, w_spatial, b_spatial, w_out):
    bsz, t, _ = h.shape
    u, v, gate = jnp.split(h @ w_in, 3, axis=-1)
    v = layernorm(v, ln_g, ln_b)
    c = min(t, CMLP_CHUNK)
    n = t // c
    mask = jnp.tril(jnp.ones((c, c), dtype=w_spatial.dtype))
    ws = w_spatial[:, :c, :c] * mask
    bs = b_spatial[:, :c]
    vg = v.reshape(bsz, n, c, CMLP_GROUPS, CMLP_GROUP_DIM)
    mixed = jnp.einsum('gts,bnsgd->bntgd', ws, vg) + bs.T[:, :, None]
    z = u * mixed.reshape(bsz, t, CMLP_WIDTH) * jax.nn.silu(gate)
    return z @ w_out, v


def trunk(x, gla_states, norm_g, gla_w_in, gla_w_a_up, gla_b_a_up, gla_g_onorm, gla_w_out,
          cmlp_w_in, cmlp_ln_g, cmlp_ln_b, cmlp_w_spatial, cmlp_b_spatial, cmlp_w_out, norm_final):
    new_gla, new_v = [], []
    for i in range(DEPTH):
        h = rmsnorm(x, norm_g[i])
        j = i // N_MIXERS
        if i % N_MIXERS == 0:
            y, s = gla_mixer(h, gla_states[j], gla_w_in[j], gla_w_a_up[j], gla_b_a_up[j],
                             gla_g_onorm[j], gla_w_out[j])
            new_gla.append(s)
        else:
            y, vrows = cmlp_mixer(h, cmlp_w_in[j], cmlp_ln_g[j], cmlp_ln_b[j], cmlp_w_spatial[j],
                                  cmlp_b_spatial[j], cmlp_w_out[j])
            new_v.append(vrows)
        x = x + y
    return rmsnorm(x, norm_final), jnp.stack(new_gla), jnp.stack(new_v)


def setup_inputs(seed: int = 0) -> dict:
    key = jax.random.key(seed)
    ks = jax.random.split(key, 20)
    f32 = jnp.float32
    nrm = lambda k, s, sc: jax.random.normal(k, s, f32) * sc
    return {
        "x_prompt": nrm(ks[0], (BATCH, SEQ, D_MODEL), 1.0),
        "x_sample": nrm(ks[1], (DEC_BATCH, DEC_SEQ, D_MODEL), 1.0),
        "state_gla": nrm(ks[2], (N_GLA, DEC_BATCH, GLA_HEADS, GLA_DK, GLA_DV), 0.5),
        "norm_g": 1.0 + nrm(ks[3], (DEPTH, D_MODEL), 0.02),
        "gla_w_in": nrm(ks[4], (N_GLA, D_MODEL, GLA_IN_WIDTH), D_MODEL ** -0.5),
        "gla_w_a_up": nrm(ks[5], (N_GLA, GLA_GATE_RANK, GLA_KW), GLA_GATE_RANK ** -0.5),
        "gla_b_a_up": nrm(ks[6], (N_GLA, GLA_KW), 0.1),
        "gla_g_onorm": 1.0 + nrm(ks[7], (N_GLA, GLA_DV), 0.02),
        "gla_w_out": nrm(ks[8], (N_GLA, GLA_VW, D_MODEL), GLA_VW ** -0.5),
        "cmlp_w_in": nrm(ks[9], (N_CMLP, D_MODEL, 3 * CMLP_WIDTH), D_MODEL ** -0.5),
        "cmlp_ln_g": 1.0 + nrm(ks[10], (N_CMLP, CMLP_WIDTH), 0.02),
        "cmlp_ln_b": nrm(ks[11], (N_CMLP, CMLP_WIDTH), 0.02),
        "cmlp_w_spatial": nrm(ks[12], (N_CMLP, CMLP_GROUPS, CMLP_CHUNK, CMLP_CHUNK), CMLP_CHUNK ** -0.5),
        "cmlp_b_spatial": 1.0 + nrm(ks[13], (N_CMLP, CMLP_GROUPS, CMLP_CHUNK), 0.02),
        "cmlp_w_out": nrm(ks[14], (N_CMLP, CMLP_WIDTH, D_MODEL), CMLP_WIDTH ** -0.5),
        "norm_final": 1.0 + nrm(ks[15], (D_MODEL,), 0.02),
    }


def reference(x_prompt, x_sample, state_gla, norm_g, gla_w_in, gla_w_a_up, gla_b_a_up, gla_g_onorm,
              gla_w_out, cmlp_w_in, cmlp_ln_g, cmlp_ln_b, cmlp_w_spatial, cmlp_b_spatial, cmlp_w_out,
              norm_final):
    weights = (norm_g, gla_w_in, gla_w_a_up, gla_b_a_up, gla_g_onorm, gla_w_out,
               cmlp_w_in, cmlp_ln_g, cmlp_ln_b, cmlp_w_spatial, cmlp_b_spatial, cmlp_w_out, norm_final)
    zero_state = jnp.zeros((N_GLA, x_prompt.shape[0], GLA_HEADS, GLA_DK, GLA_DV), jnp.float32)
    y_prompt, gla_state_prompt, _ = trunk(x_prompt, zero_state, *weights)
    y_sample, gla_state_sample, cmlp_v_sample = trunk(x_sample, state_gla, *weights)
    return (y_prompt, y_sample, gla_state_prompt, gla_state_sample, cmlp_v_sample)
```

```python
import numpy as np
from contextlib import ExitStack
import concourse.bass as bass
import concourse.mybir as mybir
from concourse.bass_utils import run_bass_kernel_spmd

F32 = mybir.dt.float32
BF16 = mybir.dt.bfloat16
U8 = mybir.dt.uint8
AF = mybir.ActivationFunctionType
ALU = mybir.AluOpType

D = 1024
H = 4
DK = 128
DV = 256
GW = 3088
NCORES = 8
DRAIN_AT = None
EPS = 1e-6
ENGS = ("pe", "act", "dve", "pool", "sp")


class Sched:
    def __init__(self):
        self.prog = {e: [] for e in ENGS}
        self.cnt = {}
        self.res = {}
        self.seen = {e: {} for e in ENGS}
        self.pending = {}
        self.snap = {}

    def _collect(self, eng, reads, writes, after):
        waits = {}

        def need(dep):
            if dep is None:
                return
            k, v = dep
            if eng == "pe" and k == "pe":
                return
            if waits.get(k, 0) < v:
                waits[k] = v

        for r in reads:
            st = self.res.get(r)
            if st:
                need(st["w"])
        for w in list(writes) + list(after):
            st = self.res.get(w)
            if st:
                need(st["w"])
                for k, v in st["r"].items():
                    need((k, v))
        out = []
        seen = self.seen[eng]
        for k, v in sorted(waits.items(), key=lambda kv: -len(self.snap.get(kv, ()))):
            if seen.get(k, 0) >= v:
                continue
            seen[k] = v
            out.append((k, v))
            for k2, v2 in self.snap.get((k, v), {}).items():
                if seen.get(k2, 0) < v2:
                    seen[k2] = v2
        return out

    def op(self, eng, fn, reads=(), writes=(), after=(), dma=None):
        after = list(after) + self.pending.pop(eng, [])
        waits = self._collect(eng, reads, writes, after)
        if dma is not None:
            key = "dma:" + dma
            inc = 16
        else:
            key = eng
            inc = 1
        val = self.cnt.get(key, 0) + inc
        self.cnt[key] = val
        self.snap[(key, val)] = dict(self.seen[eng])
        self.prog[eng].append((waits, fn, (key, inc)))
        for r in reads:
            st = self.res.setdefault(r, {"w": None, "r": {}})
            if st["r"].get(key, 0) < val:
                st["r"][key] = val
        for w in writes:
            self.res[w] = {"w": (key, val), "r": {}}
        return (key, val)

    def barrier(self, pool_fn):
        keys = [k for k in self.cnt if k.startswith("dma:")] + [k for k in self.cnt if not k.startswith("dma:") and k != "pool"]
        waits = []
        for k in keys:
            v = self.cnt[k]
            if self.seen["pool"].get(k, 0) >= v:
                continue
            self.seen["pool"][k] = v
            waits.append((k, v))
        val = self.cnt.get("pool", 0) + 1
        self.cnt["pool"] = val
        self.prog["pool"].append((waits, pool_fn, ("pool", 1)))
        for e in ENGS:
            if e == "pool":
                continue
            for k, v in self.cnt.items():
                self.seen[e][k] = max(self.seen[e].get(k, 0), v if k != "pool" else 0)
            self.seen[e]["pool"] = val
            self.prog[e].append(([("pool", val)], None, None))

    def final_wait(self, eng, keys):
        waits = [(k, self.cnt[k]) for k in keys if k in self.cnt]
        self.prog[eng].append((waits, None, None))


class Carver:
    def __init__(self, handle, base, limit):
        self.h = handle
        self.off = base
        self.limit = limit

    def alloc(self, free_shape, dtype):
        esz = 4 if dtype == F32 else 2
        n = 1
        for s in free_shape:
            n *= s
        nbytes = n * esz
        off = (self.off + 63) // 64 * 64
        assert off + nbytes <= self.limit, (off, nbytes, self.limit)
        self.off = off + nbytes
        v = self.h[:, off:off + nbytes].bitcast(dtype)
        if len(free_shape) == 2:
            v = v.rearrange("p (a b) -> p a b", a=free_shape[0])
        elif len(free_shape) == 3:
            v = v.rearrange("p (a b c) -> p a b c", a=free_shape[0], b=free_shape[1])
        return v


class Loader:
    def __init__(self):
        self.items = []
        self.pos = 0

    def add(self, key, thunk):
        self.items.append((key, thunk))

    def ensure(self, key):
        last = None
        for i in range(self.pos, len(self.items)):
            if self.items[i][0] == key:
                last = i
        if last is None:
            return
        while self.pos <= last:
            self.items[self.pos][1]()
            self.pos += 1

    def step(self):
        if self.pos < len(self.items):
            self.items[self.pos][1]()
            self.pos += 1
            return True
        return False

    def flush(self):
        while self.step():
            pass


def _double_first(g):
    try:
        next(g)
        next(g)
    except StopIteration:
        return
    yield
    for _ in g:
        yield


def _roundrobin(gens):
    gens = list(gens)
    while gens:
        nxt = []
        for g in gens:
            try:
                next(g)
                nxt.append(g)
            except StopIteration:
                pass
        gens = nxt


def build(NTP=16, SBUF_KB=207):
    NT = NTP + 1
    nc = bass.Bass("TRN2", target_bir_lowering=False)

    def din(name, shape):
        return nc.dram_tensor(name, shape, F32, kind="ExternalInput").ap()

    def dout(name, shape):
        return nc.dram_tensor(name, shape, F32, kind="ExternalOutput").ap()

    x_d = din("x", [NT * 128, D])
    st_d = din("st", [16, H, DK, DV])
    gwin_d = din("gla_w_in", [D, GW])
    gwau_d = din("gla_w_a_up", [16, 512])
    gbau_d = din("gla_b_a_up", [1, 512])
    gwout_d = din("gla_w_out", [D, D])
    cwin_d = din("cmlp_w_in", [D, 3 * D])
    clng_d = din("cmlp_ln_g", [D])
    clnb_d = din("cmlp_ln_b", [D])
    cws_d = din("cmlp_w_spatial", [4, 128, 128])
    cwout_d = din("cmlp_w_out", [D, D])
    nf_d = din("norm_final", [D])
    vec_d = din("vecrows", [22, 128])
    const_d = din("consts", [128, 1032])
    mq_d = din("mq", [1024])

    wscr = nc.dram_tensor("wscr", [128, 8 * 3 * D], BF16, kind="Internal").ap().rearrange("p (k n) -> p k n", k=8)
    wscr2 = nc.dram_tensor("wscr2", [128, 8 * D], BF16, kind="Internal").ap().rearrange("p (k n) -> p k n", k=8)
    y_d = dout("y", [NT * 128, D])
    stp_d = dout("stp", [H, DK, DV])
    sts_d = dout("sts", [16, H, DK, DV])
    vs_d = dout("vs", [128, D])

    S = Sched()
    es = ExitStack()
    TOTAL = 207360
    main = es.enter_context(nc.sbuf_tensor("main", [128, TOTAL], U8))
    banks = [es.enter_context(nc.psum_tensor("ps%d" % i, [128, 512], F32)) for i in range(8)]
    banks_bf = [b.bitcast(BF16) for b in banks]

    pa = Carver(main, 0, TOTAL)
    w_in_flat = pa.alloc([8 * GW], BF16)
    w_out = pa.alloc([8, D], BF16)
    x1all = pa.alloc([NT, D], F32)
    cst = pa.alloc([1032], F32)
    ident_bf = pa.alloc([128], BF16)
    vecT = pa.alloc([24], F32)
    ngq = pa.alloc([8], F32)
    smalls = pa.alloc([64], F32)
    waug = pa.alloc([512], BF16)
    aT = [pa.alloc([128], BF16) for _ in range(2)]
    wsT_p = pa.alloc([4, 128], BF16)
    wsT_s = pa.alloc([4, 128], BF16)
    bsT_s = pa.alloc([4], F32)
    overlay_base = pa.off

    ident = cst[:, 0:128]
    triN_p, UN_p, mask_p = cst[:, 128:256], cst[:, 256:384], cst[:, 384:512]
    triN_s, UN_s, mask_s = cst[:, 512:640], cst[:, 640:768], cst[:, 768:896]
    bmask8 = cst[:, 896:904]
    R8 = cst[:, 904:1032]
    w_in_A = w_in_flat.rearrange("p (k n) -> p k n", k=8)
    w_in_B = w_in_A[:, :, 0:3 * D]

    ALIAS0 = 6
    if NT >= 17:
        _st = [0]
        xflat = x1all[:, ALIAS0:NT, :].rearrange("p a b -> p (a b)")
        xcap = (NT - ALIAS0) * D

        def sc_alloc(free_shape, dtype):
            esz = 4 if dtype == F32 else 2
            n = 1
            for s in free_shape:
                n *= s
            nwords = (n * esz + 3) // 4
            w0 = (_st[0] + 15) // 16 * 16
            assert w0 + nwords <= xcap, (w0, nwords, xcap)
            _st[0] = w0 + nwords
            v = xflat[:, w0:w0 + nwords].bitcast(dtype)
            if len(free_shape) == 2:
                v = v.rearrange("p (a b) -> p a b", a=free_shape[0])
            elif len(free_shape) == 3:
                v = v.rearrange("p (a b c) -> p a b c", a=free_shape[0], b=free_shape[1])
            return v
        SFENCE = True
        A = Carver(main, overlay_base, TOTAL)
    else:
        XA = Carver(main, overlay_base, overlay_base + 50 * 1024)
        sc_alloc = XA.alloc
        SFENCE = False
        A = Carver(main, overlay_base + 50 * 1024, TOTAL)
    NSB = 4
    stage_blk = sc_alloc([4, 1024], F32)
    wstage = [stage_blk[:, i // 2, (i % 2) * 512:(i % 2) * 512 + 512] for i in range(8)]
    Sb = [stage_blk[:, i, :].rearrange("p (h v) -> p h v", h=4) for i in range(NSB)]
    Sb_bf = [sc_alloc([4, DV], BF16) for _ in range(2)]
    qblk = sc_alloc([2048], F32)
    qm8 = qblk.bitcast(BF16).rearrange("p (a b c) -> p a b c", a=4, b=8)
    kdm8 = sc_alloc([8, 512], BF16)
    mq = sc_alloc([8, 128], F32)
    vbf_s = sc_alloc([D], BF16)
    sg_s = sc_alloc([D], BF16)
    misc = qblk[:, 0:1280]
    wsall = misc[:, 0:512].rearrange("p (g s) -> p g s", g=4)
    T1 = misc[:, 512:640]
    vecrows = misc[:, 640:768]
    waug_st = misc[:, 768:1280]

    hA = A.alloc([D], BF16)
    hTA2 = [A.alloc([8, 128], BF16) for _ in range(2)]
    qk = [A.alloc([D], BF16) for _ in range(2)]
    vbf = [A.alloc([D], BF16) for _ in range(2)]
    sg = [A.alloc([D], BF16) for _ in range(2)]
    spb = A.alloc([512], F32)
    ecT = A.alloc([4, 128], F32)
    encT = A.alloc([4, 128], F32)
    eU = A.alloc([512], F32)
    kdec = A.alloc([512], BF16)
    qTt = A.alloc([4, 128], BF16)
    kTt = A.alloc([4, 128], BF16)
    scT = A.alloc([4, 128], BF16)
    og = A.alloc([D], BF16)
    ogT = A.alloc([8, 128], BF16)
    Sst = A.alloc([4, DV], F32)
    Sbf = A.alloc([4, DV], BF16)
    elast = A.alloc([4, 16], F32)
    sp_hi = A.alloc([512], BF16)
    sp_lo = A.alloc([512], BF16)
    cst_bf = A.alloc([4, 128], BF16)
    og_s = A.alloc([D], BF16)
    ogT_s = A.alloc([8, 128], BF16)

    B = Carver(main, overlay_base, TOTAL)
    NWB = 3
    TOPC = Carver(main, TOTAL - NWB * 4096, TOTAL)
    wstageB = [TOPC.alloc([512], F32) for _ in range(2 * NWB)]
    B.limit = TOTAL - NWB * 4096
    A.limit = TOTAL - NWB * 4096
    hB = B.alloc([D], BF16)
    _hb1 = B.alloc([8, 128], BF16)
    _hb0 = B.alloc([8, 128], BF16)
    hTB2 = [_hb0, _hb1]
    lng_bc = B.alloc([D], F32)
    lnb_bc = B.alloc([D], F32)
    gfin_bc = B.alloc([D], F32)
    ubuf = [B.alloc([D], F32) for _ in range(2)]
    sg1 = [B.alloc([D], BF16) for _ in range(2)]
    vt = [B.alloc([D], F32) for _ in range(2)]
    vn = B.alloc([D], BF16)
    zb = B.alloc([D], BF16)
    zT = B.alloc([8, 128], BF16)
    yout = [B.alloc([D], F32) for _ in range(1)]
    bnst = [B.alloc([12], F32) for _ in range(2)]
    print("SBUF bytes: persistent %d, A end %d, B end %d, total %d" % (overlay_base, A.off, B.off, TOTAL))

    sm_ctr = [0]

    def small(n=1):
        c = sm_ctr[0]
        if c + n > 48:
            c = 0
        sm_ctr[0] = c + n
        return smalls[:, c:c + n], ("sm", c, n)

    ONE_AP = smalls[:, 62:63]
    EPS_AP = smalls[:, 63:64]

    from collections import deque
    free_q = deque(range(8))

    def pbank():
        assert free_q, "out of PSUM banks"
        return free_q.popleft()

    def pfree(*bs):
        for b in bs:
            assert b not in free_q
            free_q.append(b)

    def R(b):
        return ("ps", b)

    cast_rr = [0]

    def cast_op(out, in_, scale, reads, writes, after=()):
        e = ("act", "dve")[cast_rr[0] % 2]
        cast_rr[0] += 1
        if e == "act":
            if scale is None:
                fn = lambda en: en.activation(out=out, in_=in_, func=AF.Copy)
            else:
                fn = lambda en: en.activation(out=out, in_=in_, func=AF.Identity, scale=scale)
        else:
            if scale is None:
                fn = lambda en: en.tensor_copy(out=out, in_=in_)
            else:
                fn = lambda en: en.tensor_scalar(out=out, in0=in_, scalar1=scale, scalar2=None, op0=ALU.mult)
        S.op(e, fn, reads=reads, writes=writes, after=after)

    fence_l = ["SFENCE"] if SFENCE else []

    def load_x(ti):
        wr = [("x1", ti, 0), ("x1", ti, 1)]
        if SFENCE and ti >= ALIAS0:
            wr = wr + ["SFENCE"]
        S.op("sp", lambda e: e.dma_start(out=x1all[:, ti, :], in_=x_d[ti * 128:(ti + 1) * 128, :]),
             writes=wr, dma="x%d" % ti)

    S.op("sp", lambda e: e.dma_start(out=cst[:, 0:128], in_=const_d[:, 0:128]), writes=["cst_id"], dma="c0a")
    load_x(0)
    S.op("sp", lambda e: e.dma_start(out=vecrows[0:22, :], in_=vec_d), reads=fence_l, writes=["vecrows"], dma="c1")
    S.op("sp", lambda e: e.dma_start(out=waug_st[0:16, :], in_=gwau_d), reads=fence_l, writes=["waug_st0"], dma="c2")
    S.op("sp", lambda e: e.dma_start(out=waug_st[16:17, :], in_=gbau_d), reads=fence_l, writes=["waug_st1"], dma="c3")
    S.op("sp", lambda e: e.dma_start(out=wsall[:, :, :], in_=cws_d.rearrange("g t s -> t g s")),
         reads=fence_l, writes=["wsall"], dma="c4")
    S.op("sp", lambda e: e.dma_start(out=cst[:, 128:1032], in_=const_d[:, 128:1032]), writes=["cst"], dma="c0")
    if NT > 1:
        load_x(1)

    S.op("pool", lambda e: e.memset(EPS_AP, EPS), writes=["epsc"])
    S.op("pool", lambda e: e.memset(ONE_AP, 1.0), writes=["onec"])
    S.op("dve", lambda e: e.tensor_copy(out=ident_bf[:, :], in_=ident), reads=["cst_id"], writes=["ident_bf"])
    S.op("dve", lambda e: e.tensor_copy(out=cst_bf[:, 0:2, :], in_=cst[:, 128:384].rearrange("p (a b) -> p a b", a=2)),
         reads=["cst"], writes=["cst_bf0"])
    S.op("dve", lambda e: e.tensor_copy(out=cst_bf[:, 2:4, :], in_=cst[:, 512:768].rearrange("p (a b) -> p a b", a=2)),
         reads=["cst"], writes=["cst_bf1"])
    S.op("dve", lambda e: e.tensor_copy(out=waug[0:17, :], in_=waug_st[0:17, :]),
         reads=["waug_st0", "waug_st1"] + fence_l, writes=["waug"])
    for i in range(2):
        S.op("pool", lambda e, i=i: e.memset(aT[i][0:17, :], 1.0), writes=[("aT", i)])
    b = pbank()
    S.op("pe", lambda e, b=b: e.transpose(out=banks[b][:, 0:22], in_=vecrows[0:22, :], identity=cst[0:22, 0:22]),
         reads=["vecrows", "cst_id"] + fence_l, writes=[R(b)])
    S.op("dve", lambda e, b=b: e.tensor_copy(out=vecT[:, 0:22], in_=banks[b][:, 0:22]), reads=[R(b)], writes=["vecT"])
    pfree(b)
    S.op("dve", lambda e: e.tensor_scalar(out=ngq[:, 0:8], in0=vecT[:, 0:8], scalar1=float(DK ** -0.5),
                                          scalar2=None, op0=ALU.mult), reads=["vecT"], writes=["ngq"])
    bsT_p = vecT[:, 18:22]

    wst_ctr = [0]

    def load_weight_chunk(src_d, kt, c0, ncols, dst3, scale8, scale_res, resname, stages, sname, fence, after=()):
        i = wst_ctr[0] % len(stages)
        n = wst_ctr[0]
        wst_ctr[0] += 1
        stg = stages[i]
        srcv = src_d.rearrange("(k p) n -> p k n", p=128)[:, kt, c0:c0 + ncols]
        S.op("sp", lambda e: e.dma_start(out=stg[:, 0:ncols], in_=srcv),
             reads=fence, writes=[(sname, i)], dma="%s%d" % (sname, i))
        wr = [(resname, c0 // 512 * 512, kt)]
        rd = [(sname, i)] + scale_res + fence
        dst = dst3[:, kt, c0:c0 + ncols]
        if n % 2 == 0:
            if scale8 is None:
                S.op("dve", lambda e: e.tensor_copy(out=dst, in_=stg[:, 0:ncols]), reads=rd, writes=wr, after=after)
            else:
                S.op("dve", lambda e: e.tensor_tensor(out=dst, in0=stg[:, 0:ncols],
                                                      in1=scale8[:, kt:kt + 1].to_broadcast([128, ncols]),
                                                      op=ALU.mult), reads=rd, writes=wr, after=after)
        else:
            if scale8 is None:
                S.op("act", lambda e: e.activation(out=dst, in_=stg[:, 0:ncols], func=AF.Copy), reads=rd, writes=wr, after=after)
            else:
                S.op("act", lambda e: e.activation(out=dst, in_=stg[:, 0:ncols], func=AF.Identity,
                                                   scale=scale8[:, kt:kt + 1]), reads=rd, writes=wr, after=after)

    def load_alow_chunk():
        i = wst_ctr[0] % 8
        wst_ctr[0] += 1
        stg = wstage[i][:, 0:128].rearrange("p (k n) -> p k n", k=8)
        srcv = gwin_d.rearrange("(k p) n -> p k n", p=128)[:, :, 3072:3088]
        S.op("sp", lambda e: e.dma_start(out=stg, in_=srcv), reads=fence_l, writes=[("wstA", i)], dma="wstA%d" % i)
        S.op("dve", lambda e: e.tensor_tensor(out=w_in_A[:, :, 3072:3088], in0=stg,
                                              in1=vecT[:, 0:8].unsqueeze(2).to_broadcast([128, 8, 16]), op=ALU.mult),
             reads=[("wstA", i), "vecT"] + fence_l, writes=[("wA_in", 3072, kp) for kp in range(8)])

    gon8 = smalls[:, 48:56]
    S.op("dve", lambda e: e.tensor_copy(out=gon8.rearrange("p (a b) -> p a b", b=2),
                                        in_=vecT[:, 16:18].unsqueeze(1).to_broadcast([128, 4, 2])),
         reads=["vecT"], writes=["gon8"])
    LD = Loader()
    loaders = {"wA_in": LD, "wA_out": LD}
    for cb in range(6):
        for kp in range(8):
            LD.add(("wA_in", cb * 512), lambda kp=kp, cb=cb: load_weight_chunk(
                gwin_d, kp, cb * 512, 512, w_in_A, ngq[:, 0:8] if cb == 0 else vecT[:, 0:8], ["ngq", "vecT"],
                "wA_in", wstage, "wstA", fence_l))
        if cb == 1:
            LD.add(("wA_in", 3072), load_alow_chunk)
    for cb in range(2):
        for kp in range(8):
            LD.add(("wA_out", cb * 512), lambda kp=kp, cb=cb: load_weight_chunk(
                gwout_d, kp, cb * 512, 512, w_out, gon8, ["gon8"], "wA_out", wstage, "wstA", fence_l))

    def wres(name, c0, c1, cw=128):
        return [(name, cb, kp) for cb in range(c0 // 512 * 512, c1, 512) for kp in range(8)]

    LDB = Loader()
    PRECAST = NT >= 12

    def swap_block(cb):
        S.op("sp", lambda e: e.dma_start(out=w_in_B[:, :, cb * 512:(cb + 1) * 512], in_=wscr[:, :, cb * 512:(cb + 1) * 512]),
             reads=[("wscr", cb, kt) for kt in range(8)], writes=[("wB_in", cb * 512, kt) for kt in range(8)],
             after=wres("wA_in", cb * 512, cb * 512 + 512), dma="swp%d" % cb)

    for cb in range(6):
        if PRECAST:
            LDB.add(("wB_in", cb * 512), lambda cb=cb: swap_block(cb))
        else:
            for kp in range(8):
                LDB.add(("wB_in", cb * 512), lambda kp=kp, cb=cb: load_weight_chunk(
                    cwin_d, kp, cb * 512, 512, w_in_B, vecT[:, 8:16], ["vecT"], "wB_in", wstageB, "wstB", [],
                    after=wres("wA_in", cb * 512, cb * 512 + 512)))

    def precast_bg():
        chunks = [(cb, kt) for cb in range(8) for kt in range(8)]
        NCH = len(chunks)

        def stage(n):
            i = n % 4
            return i, wstageB[i], wstageB[4 + i // 2].bitcast(BF16)[:, (i % 2) * 512:(i % 2) * 512 + 512]

        for r in range(NCH + 5):
            if r < NCH:
                cb, kt = chunks[r]
                i, stg, ost = stage(r)
                if cb < 6:
                    srcv = cwin_d.rearrange("(k p) n -> p k n", p=128)[:, kt, cb * 512:(cb + 1) * 512]
                else:
                    srcv = cwout_d.rearrange("(k p) n -> p k n", p=128)[:, kt, (cb - 6) * 512:(cb - 5) * 512]
                S.op("sp", lambda e, stg=stg, srcv=srcv: e.dma_start(out=stg[:, :], in_=srcv),
                     writes=[("pci", i)], dma="pci%d" % i)
            n = r - 3
            if 0 <= n < NCH:
                cb, kt = chunks[n]
                i, stg, ost = stage(n)
                sc = vecT[:, 8 + kt:9 + kt]
                if cb >= 6:
                    if n % 2 == 0:
                        S.op("dve", lambda e, stg=stg, ost=ost: e.tensor_copy(out=ost, in_=stg[:, :]),
                             reads=[("pci", i)], writes=[("pco", i)])
                    else:
                        S.op("act", lambda e, stg=stg, ost=ost: e.activation(out=ost, in_=stg[:, :], func=AF.Copy),
                             reads=[("pci", i)], writes=[("pco", i)])
                elif n % 2 == 0:
                    S.op("dve", lambda e, stg=stg, ost=ost, sc=sc: e.tensor_tensor(
                        out=ost, in0=stg[:, :], in1=sc.to_broadcast([128, 512]), op=ALU.mult),
                        reads=[("pci", i), "vecT"], writes=[("pco", i)])
                else:
                    S.op("act", lambda e, stg=stg, ost=ost, sc=sc: e.activation(out=ost, in_=stg[:, :], func=AF.Identity, scale=sc),
                         reads=[("pci", i), "vecT"], writes=[("pco", i)])
            n = r - 5
            if 0 <= n < NCH:
                cb, kt = chunks[n]
                i, stg, ost = stage(n)
                dstv = wscr[:, kt, cb * 512:(cb + 1) * 512] if cb < 6 else wscr2[:, kt, (cb - 6) * 512:(cb - 5) * 512]
                S.op("sp", lambda e, ost=ost, dstv=dstv: e.dma_start(out=dstv, in_=ost),
                     reads=[("pco", i)], writes=[("wscr", cb, kt)], dma="pco%d" % i)
            yield

    LD.ensure(("wA_in", 0))

    for g in range(4):
        b = pbank()
        S.op("pe", lambda e, b=b, g=g: e.transpose(out=banks[b][:, 0:128], in_=wsall[:, g, :], identity=ident),
             reads=["wsall", "cst", "cst_id"] + fence_l, writes=[R(b)])
        S.op("dve", lambda e, b=b, g=g: e.tensor_tensor(out=wsT_p[:, g, :], in0=banks[b][:, 0:128], in1=mask_p,
                                                         op=ALU.mult), reads=[R(b), "cst"], writes=[("wsT_p", g)])
        pfree(b)
        b = pbank()
        S.op("pe", lambda e, b=b, g=g: e.matmul(banks[b][0:8, 0:128], lhsT=wsall[0:8, g, 0:8], rhs=R8[0:8, :],
                                                start=True, stop=True),
             reads=["wsall", "cst"] + fence_l, writes=[R(b)])
        S.op("dve", lambda e, b=b: e.tensor_copy(out=T1[0:8, :], in_=banks[b][0:8, 0:128]), reads=[R(b)] + fence_l, writes=["T1"])
        pfree(b)
        b = pbank()
        S.op("pe", lambda e, b=b: e.matmul(banks[b][:, 0:128], lhsT=R8[0:8, :], rhs=T1[0:8, :], start=True, stop=True),
             reads=["T1", "cst"] + fence_l, writes=[R(b)])
        S.op("dve", lambda e, b=b, g=g: e.tensor_tensor(out=wsT_s[:, g, :], in0=banks[b][:, 0:128], in1=mask_s,
                                                         op=ALU.mult), reads=[R(b), "cst"], writes=[("wsT_s", g)])
        pfree(b)
    b = pbank()
    S.op("pe", lambda e, b=b: e.matmul(banks[b][:, 0:4], lhsT=R8[0:8, :], rhs=vecT[0:8, 18:22], start=True, stop=True),
         reads=["vecT", "cst"], writes=[R(b)])
    S.op("dve", lambda e, b=b: e.tensor_copy(out=bsT_s[:, 0:4], in_=banks[b][:, 0:4]), reads=[R(b)], writes=["bsT_s"])
    pfree(b)

    S.op("sp", lambda e: e.dma_start(out=mq.rearrange("p a b -> p (a b)"), in_=mq_d.partition_broadcast(128)),
         reads=fence_l, writes=["mq"], dma="c5")

    def rstd_ops(ssq_ap, ssq_res, n_elems, width):
        lnv, lnv_r = small(width)
        rs, rs_r = small(width)
        S.op("act", lambda e: e.activation(out=lnv, in_=ssq_ap, func=AF.Ln, scale=1.0 / n_elems, bias=EPS_AP),
             reads=list(ssq_res) + ["epsc"], writes=[lnv_r])
        S.op("act", lambda e: e.activation(out=rs, in_=lnv, func=AF.Exp, scale=-0.5), reads=[lnv_r], writes=[rs_r])
        return rs, rs_r

    NEGHALF = smalls[:, 57:61]
    S.op("pool", lambda e: e.memset(NEGHALF, -0.5), writes=["neghalf"])

    def rstd_ops_pool(ssq_ap, ssq_res, n_elems, width):
        tmp, tmp_r = small(width)
        rs, rs_r = small(width)
        S.op("pool", lambda e: e.tensor_scalar(out=tmp, in0=ssq_ap, scalar1=1.0 / n_elems, scalar2=EPS, op0=ALU.mult, op1=ALU.add),
             reads=list(ssq_res), writes=[tmp_r])
        S.op("pool", lambda e: e.tensor_tensor(out=rs, in0=tmp, in1=NEGHALF[:, 0:width], op=ALU.pow),
             reads=[tmp_r, "neghalf"], writes=[rs_r])
        return rs, rs_r

    def transposes8(src, src_res, dstT, dst_res, evac_eng):
        b = pbank()
        pv = banks_bf[b]

        def fn(e):
            ins = None
            for kt in range(8):
                ins = e.transpose(out=pv[:, kt * 128:(kt + 1) * 128], in_=src[:, kt * 128:(kt + 1) * 128],
                                  identity=ident_bf[:, :])
            return ins
        S.op("pe", fn, reads=list(src_res) + ["ident_bf"], writes=[R(b)])
        dflat = dstT.rearrange("p a b -> p (a b)")
        if evac_eng == "act":
            S.op("act", lambda e: e.activation(out=dflat, in_=pv[:, 0:1024], func=AF.Copy), reads=[R(b)], writes=[dst_res])
        else:
            S.op(evac_eng, lambda e: e.tensor_copy(out=dflat, in_=pv[:, 0:1024]), reads=[R(b)], writes=[dst_res])
        pfree(b)

    def proj_chunk(hT, hT_res, w3, wname, c0, ncols, cw=256, last_tile=False):
        loaders[wname].ensure((wname, c0 // 512 * 512))
        for _ in range(2 if (PRECAST and wname.startswith("wB")) else 8):
            loaders[wname].step()
        b = pbank()

        def fn(e):
            ins = None
            for kt in range(8):
                ins = e.matmul(banks[b][:, 0:ncols], lhsT=hT[:, kt, :], rhs=w3[:, kt, c0:c0 + ncols],
                               start=(kt == 0), stop=(kt == 7))
            return ins
        S.op("pe", fn, reads=[hT_res] + wres(wname, c0, c0 + ncols, cw), writes=[R(b)])
        if wname == "wA_in" and last_tile:
            swap_state["allowed"] = c0 // 512
        return b
    swap_state = {"allowed": -1}

    def hsl(h):
        return slice(h * 256, (h + 1) * 256)

    def gla_front_a(ti):
        xi = x1all[:, ti, :]
        xr = [("x1", ti, 0), ("x1", ti, 1)]
        ssq, ssq_r = small()
        S.op("act", lambda e: e.activation(out=hA[:, :], in_=xi, func=AF.Square, accum_out=ssq),
             reads=xr, writes=["hA", ssq_r])
        rs, rs_r = rstd_ops(ssq, [ssq_r], D, 1)
        S.op("act", lambda e: e.activation(out=hA[:, :], in_=xi, func=AF.Identity, scale=rs),
             reads=xr + [rs_r], writes=["hA"])

    def gla_front_b(ti):
        transposes8(hA, ["hA"], hTA2[ti % 2], ("hTA", ti % 2), "dve")

    def gla_C1_front(ti):
        par = ti % 2
        b = pbank()
        S.op("pe", lambda e, b=b: e.matmul(banks[b][:, 0:512], lhsT=aT[par][0:17, :], rhs=waug[0:17, :], start=True, stop=True),
             reads=[("aT", par), "waug"], writes=[R(b)])
        S.op("act", lambda e, b=b: e.activation(out=spb[:, :], in_=banks[b][:, 0:512], func=AF.Exp, scale=-1.0),
             reads=[R(b)], writes=["spb"])
        pfree(b)
        S.op("act", lambda e: e.activation(out=spb[:, :], in_=spb[:, :], func=AF.Ln, bias=ONE_AP), reads=["spb", "onec"], writes=["spb"])
        S.op("dve", lambda e: e.tensor_copy(out=sp_hi[:, :], in_=spb[:, :]), reads=["spb"], writes=["sp_hi"])
        S.op("dve", lambda e: e.tensor_tensor(out=sp_lo[:, :], in0=spb[:, :], in1=sp_hi[:, :], op=ALU.subtract),
             reads=["spb", "sp_hi"], writes=["sp_lo"])

    PAIR_LAST = False

    def gla_P(ti, fronts=True, c1front=True):
        par = ti % 2
        samp = (ti == 0)
        lt = (ti == NT - 1)
        sf = fence_l if samp else []
        v_dst, v_res = (vbf_s, "vbf_s") if samp else (vbf[par], ("vbf", par))
        g_dst, g_res = (sg_s, "sg_s") if samp else (sg[par], ("sg", par))
        if lt and PAIR_LAST:
            v_dst, v_res, g_dst, g_res = og_s, "vbf3", ogT_s.rearrange("p a b -> p (a b)"), "sg3"
        hTA = hTA2[par]
        hTA_r = ("hTA", par)
        b = proj_chunk(hTA, hTA_r, w_in_A, "wA_in", 0, 512, last_tile=lt)
        S.op("dve", lambda e, b=b: e.tensor_copy(out=qk[par][:, 0:512], in_=banks[b][:, 0:512]),
             reads=[R(b)], writes=[("qk", par, 0)])
        pfree(b)
        if fronts and ti + 1 < NT:
            gla_front_a(ti + 1)
        yield
        b = proj_chunk(hTA, hTA_r, w_in_A, "wA_in", 512, 512, last_tile=lt)
        S.op("dve", lambda e, b=b: e.tensor_copy(out=qk[par][:, 512:1024], in_=banks[b][:, 0:512]),
             reads=[R(b)], writes=[("qk", par, 1)])
        pfree(b)
        yield
        LD.ensure(("wA_in", 3072))
        b = pbank()

        def fn(e, b=b):
            ins = None
            for kt in range(8):
                ins = e.matmul(banks[b][0:16, 0:128], lhsT=w_in_A[:, kt, 3072:3088], rhs=hTA[:, kt, :],
                               start=(kt == 0), stop=(kt == 7))
            return ins
        S.op("pe", fn, reads=[hTA_r] + wres("wA_in", 3072, 3088), writes=[R(b)])
        S.op("dve", lambda e, b=b: e.tensor_copy(out=aT[par][0:16, :], in_=banks[b][0:16, 0:128]),
             reads=[R(b)], writes=[("aT", par)])
        pfree(b)
        yield
        if fronts and ti + 1 < NT:
            gla_front_b(ti + 1)
            yield
        if c1front:
            gla_C1_front(ti)
            yield
        for j in range(2):
            b = proj_chunk(hTA, hTA_r, w_in_A, "wA_in", 1024 + j * 512, 512, last_tile=lt)
            S.op("dve", lambda e, b=b, j=j: e.tensor_copy(out=v_dst[:, j * 512:(j + 1) * 512], in_=banks[b][:, 0:512]),
                 reads=[R(b)] + sf, writes=[(v_res, j)])
            pfree(b)
            yield
        gb = [proj_chunk(hTA, hTA_r, w_in_A, "wA_in", 2048 + j * 512, 512, last_tile=lt) for j in range(2)]
        for j in range(2):
            S.op("act", lambda e, b=gb[j], j=j: e.activation(out=g_dst[:, j * 512:(j + 1) * 512], in_=banks[b][:, 0:512],
                                                            func=AF.Silu), reads=[R(gb[j])] + sf, writes=[(g_res, j)])
        pfree(*gb)
        S.op("act", lambda e: e.activation(out=smalls[:, 56:57], in_=ONE_AP, func=AF.Exp), reads=["onec"], writes=["dummy"])
        yield

    def gla_C1(ti):
        par = ti % 2
        samp = (ti == 0)
        msk = mask_s if samp else mask_p
        triN = cst_bf[:, 2, :] if samp else cst_bf[:, 0, :]
        UN = cst_bf[:, 3, :] if samp else cst_bf[:, 1, :]
        sf = fence_l if samp else []
        bc = pbank()

        def fn(e, bc=bc):
            ins = None
            for h in range(4):
                ins = e.matmul(banks[bc][:, h * 128:(h + 1) * 128], lhsT=sp_hi[:, h * 128:(h + 1) * 128], rhs=triN,
                               start=True, stop=False)
                ins = e.matmul(banks[bc][:, h * 128:(h + 1) * 128], lhsT=sp_lo[:, h * 128:(h + 1) * 128], rhs=triN,
                               start=False, stop=True)
            return ins
        S.op("pe", fn, reads=["sp_hi", "sp_lo", "cst_bf0", "cst_bf1"], writes=[R(bc)])
        S.op("act", lambda e, bc=bc: e.activation(out=ecT.rearrange("p a b -> p (a b)"), in_=banks[bc][:, 0:512], func=AF.Exp),
             reads=[R(bc)], writes=["ecT"])
        S.op("act", lambda e, bc=bc: e.activation(out=encT.rearrange("p a b -> p (a b)"), in_=banks[bc][:, 0:512], func=AF.Exp,
                                                 scale=-1.0), reads=[R(bc)], writes=["encT"])
        pfree(bc)
        bu = pbank()
        def fn(e, bu=bu):
            e.matmul(banks[bu][:, 0:512], lhsT=UN, rhs=sp_hi[:, :], start=True, stop=False)
            return e.matmul(banks[bu][:, 0:512], lhsT=UN, rhs=sp_lo[:, :], start=False, stop=True)
        S.op("pe", fn, reads=["sp_hi", "sp_lo", "cst_bf0", "cst_bf1"], writes=[R(bu)])
        S.op("act", lambda e, bu=bu: e.activation(out=eU[:, :], in_=banks[bu][:, 0:512], func=AF.Exp), reads=[R(bu)], writes=["eU"])
        pfree(bu)
        yield
        bt = pbank()
        pv = banks_bf[bt]

        def fn(e, pv=pv):
            ins = None
            for j in range(8):
                ins = e.transpose(out=pv[:, j * 128:(j + 1) * 128], in_=qk[par][:, j * 128:(j + 1) * 128], identity=ident_bf[:, :])
            return ins
        S.op("pe", fn, reads=[("qk", par, 0), ("qk", par, 1), "ident_bf"], writes=[R(bt)])
        S.op("dve", lambda e, pv=pv: e.tensor_tensor(out=qTt.rearrange("p a b -> p (a b)"), in0=pv[:, 0:512],
                                                     in1=ecT.rearrange("p a b -> p (a b)"), op=ALU.mult),
             reads=[R(bt), "ecT"], writes=["qTt"])
        S.op("dve", lambda e, pv=pv: e.tensor_tensor(out=kTt.rearrange("p a b -> p (a b)"), in0=pv[:, 512:1024],
                                                     in1=encT.rearrange("p a b -> p (a b)"), op=ALU.mult),
             reads=[R(bt), "encT"], writes=["kTt"])
        pfree(bt)
        S.op("dve", lambda e: e.tensor_tensor(out=kdec[:, :], in0=qk[par][:, 512:1024], in1=eU[:, :], op=ALU.mult),
             reads=[("qk", par, 1), "eU"], writes=["kdec"])
        yield
        bs_ = pbank()

        def fn(e, bs_=bs_):
            ins = None
            for h in range(4):
                ins = e.matmul(banks[bs_][:, h * 128:(h + 1) * 128], lhsT=kTt[:, h, :], rhs=qTt[:, h, :], start=True, stop=True)
            return ins
        S.op("pe", fn, reads=["kTt", "qTt"], writes=[R(bs_)])
        S.op("dve", lambda e, bs_=bs_: e.tensor_tensor(
            out=scT[:, :, :], in0=banks[bs_][:, 0:512].rearrange("p (a b) -> p a b", a=4),
            in1=msk.unsqueeze(1).to_broadcast([128, 4, 128]), op=ALU.mult),
            reads=[R(bs_), "cst"], writes=["scT"])
        pfree(bs_)
        yield
        if samp:
            ob = [pbank(), pbank()]
            gla_C1.ob = ob

            def fn(e):
                ins = None
                for h in range(4):
                    o_ap = banks[ob[h // 2]][:, (h % 2) * 256:(h % 2) * 256 + 256]
                    ins = e.matmul(o_ap, lhsT=scT[:, h, :], rhs=vbf_s[:, hsl(h)], start=(h % 2 == 0), stop=False, skip_group_check=True)
                return ins
            S.op("pe", fn, reads=["scT", ("vbf_s", 0), ("vbf_s", 1)] + sf, writes=[R(ob[0]), R(ob[1])])
            S.op("dve", lambda e: e.tensor_tensor(
                out=qm8[:, :, :, :], in0=qTt.unsqueeze(2).to_broadcast([128, 4, 8, 128]),
                in1=mq.unsqueeze(1).to_broadcast([128, 4, 8, 128]), op=ALU.mult),
                reads=["qTt", "mq"] + sf, writes=["qm8"], after=["wsall", "T1", "vecrows", "waug_st0", "waug_st1"])
            S.op("dve", lambda e: e.tensor_tensor(
                out=kdm8[:, :, :], in0=kdec.unsqueeze(1).to_broadcast([128, 8, 512]),
                in1=bmask8.unsqueeze(2).to_broadcast([128, 8, 512]), op=ALU.mult),
                reads=["kdec", "cst"] + sf, writes=["kdm8"])
            S.op("dve", lambda e: e.tensor_copy(out=elast[:, :, :],
                                                in_=ecT.rearrange("p h (b t) -> p h b t", t=8)[:, :, :, 7]),
                 reads=["ecT"], writes=["elast"])
            yield

    def out_stage(ti, ob, sg_t, sg_res, sf, og=og, ogT=ogT, tag=""):
        ssq4, ssq4_r = small(4)
        for h in range(4):
            o_ap = banks[ob[h // 2]][:, (h % 2) * 256:(h % 2) * 256 + 256]
            S.op("act", lambda e, h=h, o_ap=o_ap: e.activation(out=og[:, hsl(h)], in_=o_ap, func=AF.Square,
                                                              accum_out=ssq4[:, h:h + 1]),
                 reads=[R(ob[h // 2])], writes=[("og" + tag, h), (ssq4_r, h)])
        rs4, rs4_r = rstd_ops(ssq4, [(ssq4_r, h) for h in range(4)], DV, 4)
        for h in range(4):
            o_ap = banks[ob[h // 2]][:, (h % 2) * 256:(h % 2) * 256 + 256]
            S.op("dve", lambda e, h=h, o_ap=o_ap: e.scalar_tensor_tensor(
                out=og[:, hsl(h)], in0=o_ap, scalar=rs4[:, h:h + 1], in1=sg_t[:, hsl(h)], op0=ALU.mult, op1=ALU.mult),
                reads=[R(ob[h // 2]), rs4_r, (sg_res, h // 2)] + sf, writes=[("og" + tag, h)])
        pfree(*ob)
        yield
        transposes8(og, [("og" + tag, h) for h in range(4)], ogT, "ogT" + tag, "dve")
        yield
        for j in range(2):
            b = proj_chunk(ogT, "ogT" + tag, w_out, "wA_out", j * 512, 512)
            sl = slice(j * 512, (j + 1) * 512)
            S.op("dve", lambda e, b=b, sl=sl: e.tensor_tensor(out=x1all[:, ti, sl], in0=x1all[:, ti, sl], in1=banks[b][:, 0:512],
                                                              op=ALU.add), reads=[R(b), ("x1", ti, j)], writes=[("x1", ti, j)])
            pfree(b)
            yield

    def gla_C2(ti, first_prompt, last):
        par = ti % 2
        vres = [(("vbf", par), 0), (("vbf", par), 1)]
        v_t, sg_t, sg_r = vbf[par], sg[par], ("sg", par)
        if ti == NT - 1 and PAIR_LAST:
            vres = [("vbf3", 0), ("vbf3", 1)]
            v_t, sg_t, sg_r = og_s, ogT_s.rearrange("p a b -> p (a b)"), "sg3"
        ob = [pbank(), pbank()]

        def fn(e):
            ins = None
            for h in range(4):
                o_ap = banks[ob[h // 2]][:, (h % 2) * 256:(h % 2) * 256 + 256]
                ins = e.matmul(o_ap, lhsT=scT[:, h, :], rhs=v_t[:, hsl(h)],
                               start=(h % 2 == 0), stop=first_prompt, skip_group_check=True)
                if not first_prompt:
                    ins = e.matmul(o_ap, lhsT=qTt[:, h, :], rhs=Sbf[:, h, :], start=False, stop=True, skip_group_check=True)
            return ins
        S.op("pe", fn, reads=["scT", "qTt", "Sbf"] + vres, writes=[R(ob[0]), R(ob[1])])
        sb_ = [pbank(), pbank()]

        def fn(e):
            ins = None
            for h in range(4):
                ins = e.matmul(banks[sb_[h // 2]][:, (h % 2) * 256:(h % 2) * 256 + 256], lhsT=kdec[:, h * 128:(h + 1) * 128],
                               rhs=v_t[:, hsl(h)], start=(h % 2 == 0), stop=True, skip_group_check=True)
            return ins
        S.op("pe", fn, reads=["kdec"] + vres, writes=[R(sb_[0]), R(sb_[1])])
        for h in range(4):
            s_ps = banks[sb_[h // 2]][:, (h % 2) * 256:(h % 2) * 256 + 256]
            if first_prompt:
                S.op("dve", lambda e, h=h, s_ps=s_ps: e.tensor_copy(out=Sst[:, h, :], in_=s_ps),
                     reads=[R(sb_[h // 2])], writes=[("Sst", h)])
            else:
                S.op("dve", lambda e, h=h, s_ps=s_ps: e.scalar_tensor_tensor(
                    out=Sst[:, h, :], in0=Sst[:, h, :], scalar=ecT[:, h, 127:128], in1=s_ps, op0=ALU.mult, op1=ALU.add),
                    reads=[R(sb_[h // 2]), "ecT", ("Sst", h)], writes=[("Sst", h)])
        pfree(*sb_)
        if last:
            S.op("sp", lambda e: e.dma_start(out=stp_d.rearrange("h d v -> d h v"), in_=Sst[:, :, :]),
                 reads=[("Sst", h) for h in range(4)], dma="stp")
        else:
            S.op("dve", lambda e: e.tensor_copy(out=Sbf.rearrange("p a b -> p (a b)"), in_=Sst.rearrange("p a b -> p (a b)")),
                 reads=[("Sst", h) for h in range(4)], writes=["Sbf"])
        for _ in out_stage(ti, ob, sg_t, sg_r, []):
            yield

    def sample_bg():
        sf = fence_l
        ob = gla_C1.ob
        vres = [("vbf_s", 0), ("vbf_s", 1)]
        stage_res = [("wstA", i) for i in range(8)]

        LD.flush()

        def load_S(bq):
            i = bq % NSB
            S.op("sp", lambda e: e.dma_start(out=Sb[i][:, :, :], in_=st_d[bq].rearrange("h d v -> d h v")),
                 reads=sf, writes=[("Sb", i)], after=stage_res, dma="sbin%d" % i)
        def do_cast(bq):
            i, i2 = bq % NSB, bq % 2
            cast_op(Sb_bf[i2].rearrange("p a b -> p (a b)"), Sb[i].rearrange("p a b -> p (a b)"), None,
                    reads=[("Sb", i)] + sf, writes=[("Sb_bf", i2)])

        for bq in range(min(NSB - 1, 16)):
            load_S(bq)
        do_cast(0)
        for bq in range(16):
            i = bq % NSB
            i2 = bq % 2
            jj, bb = bq // 8, bq % 8
            if bq + NSB - 1 < 16:
                load_S(bq + NSB - 1)
            sb_ = [pbank(), pbank()]

            def fn(e, jj=jj, bb=bb, sb_=sb_):
                ins = None
                for h in range(4):
                    ins = e.matmul(banks[sb_[h // 2]][:, (h % 2) * 256:(h % 2) * 256 + 256],
                                   lhsT=kdm8[64 * jj:64 * jj + 64, bb, h * 128:(h + 1) * 128],
                                   rhs=vbf_s[64 * jj:64 * jj + 64, hsl(h)],
                                   start=(h % 2 == 0), stop=True, skip_group_check=True)
                return ins
            S.op("pe", fn, reads=["kdm8"] + vres + sf, writes=[R(sb_[0]), R(sb_[1])])
            if bq + 1 < 16:
                do_cast(bq + 1)
            bq_last = (bq == 15)

            def fn(e, i2=i2, jj=jj, bb=bb, bq_last=bq_last):
                ins = None
                for h in range(4):
                    o_ap = banks[ob[h // 2]][64 * jj:64 * jj + 64, (h % 2) * 256:(h % 2) * 256 + 256]
                    ins = e.matmul(o_ap, lhsT=qm8[:, h, bb, 64 * jj:64 * jj + 64], rhs=Sb_bf[i2][:, h, :],
                                   start=False, stop=(bq_last and h == 3), skip_group_check=True)
                return ins
            S.op("pe", fn, reads=["qm8", ("Sb_bf", i2)] + sf, writes=[R(ob[0]), R(ob[1])])
            for h in range(4):
                s_ps = banks[sb_[h // 2]][:, (h % 2) * 256:(h % 2) * 256 + 256]
                S.op("dve", lambda e, h=h, s_ps=s_ps, i=i, bq=bq: e.scalar_tensor_tensor(
                    out=Sb[i][:, h, :], in0=Sb[i][:, h, :], scalar=elast[:, h, bq:bq + 1], in1=s_ps,
                    op0=ALU.mult, op1=ALU.add),
                    reads=[R(sb_[h // 2]), "elast", ("Sb", i)] + sf, writes=[("Sb", i)])
            pfree(*sb_)
            S.op("pool", lambda e, bq=bq, i=i: e.dma_start(out=sts_d[bq].rearrange("h d v -> d h v"), in_=Sb[i][:, :, :]),
                 reads=[("Sb", i)] + sf, dma="sbout%d" % i)
            yield
        for _ in out_stage(0, ob, sg_s, "sg_s", sf, og=og_s, ogT=ogT_s, tag="_s"):
            yield

    def run_steps(step_fn, nsteps, bgs=()):
        live_bg = []
        for s in range(nsteps):
            for bgd in bgs:
                if bgd.get("gen") is not None and bgd.get("drain_at") is not None and s >= bgd["drain_at"]:
                    for _ in bgd["gen"]:
                        pass
                    bgd["gen"] = None
            gens = step_fn(s)
            for bgd in bgs:
                if s == bgd["start"]:
                    bgd["gen"] = bgd["fn"]()
            live = list(gens)
            while live:
                nxt = []
                for g in live:
                    try:
                        next(g)
                        nxt.append(g)
                    except StopIteration:
                        pass
                live = nxt
                for bgd in bgs:
                    if bgd.get("gen") is not None:
                        try:
                            next(bgd["gen"])
                        except StopIteration:
                            bgd["gen"] = None
        for bgd in bgs:
            if bgd.get("gen") is not None and not bgd.get("keep"):
                for _ in bgd["gen"]:
                    pass
                bgd["gen"] = None

    if NT > 1:
        S.op("pool", lambda e: e.memset(Sst[:, :, :], 0.0), writes=[("Sst", h) for h in range(4)])

    def stepA(s):
        gens = []
        if s + 2 < NT:
            load_x(s + 2)
        t2 = s - 2
        if 1 <= t2 < NT:
            gens.append(gla_C2(t2, first_prompt=(t2 == 1), last=(t2 == NT - 1)))
        t1 = s - 1
        if 0 <= t1 < NT:
            gens.append(gla_C1(t1))
        if PAIR_LAST and s == NT - 2:
            gens.append(_double_first(gla_P(s)))
            gens.append(_delayed(gla_P(NT - 1, fronts=False, c1front=False), 2))
        elif PAIR_LAST and s == NT - 1:
            gens.append(_one(lambda: gla_C1_front(NT - 1)))
        elif s < NT:
            gens.append(_double_first(gla_P(s)) if s >= 2 else gla_P(s))
        return gens

    def _delayed(g, n):
        for _ in range(n):
            yield
        for _ in g:
            yield

    def _one(fn):
        fn()
        yield

    def swap_bg():
        wst_ctr[0] = 0
        while LDB.pos < len(LDB.items):
            for _ in range(1 if PRECAST else 2):
                if LDB.pos < len(LDB.items) and LDB.items[LDB.pos][0][1] // 512 <= swap_state["allowed"]:
                    LDB.step()
            yield

    gla_front_a(0)
    gla_front_b(0)
    bg_s = {"start": 2, "fn": sample_bg, "drain_at": DRAIN_AT if DRAIN_AT is not None else ALIAS0 - 2}
    bg_w = {"start": NT - 2 if PAIR_LAST else NT - 1, "fn": swap_bg, "drain_at": None, "keep": True}
    def mlp_front_a(ti):
        x1 = x1all[:, ti, :]
        x1r = [("x1", ti, 0), ("x1", ti, 1)]
        ssq, ssq_r = small()
        S.op("act", lambda e: e.activation(out=hB[:, :], in_=x1, func=AF.Square, accum_out=ssq),
             reads=x1r, writes=["hB", ssq_r])
        rs, rs_r = rstd_ops_pool(ssq, [ssq_r], D, 1)
        S.op("act", lambda e: e.activation(out=hB[:, :], in_=x1, func=AF.Identity, scale=rs),
             reads=x1r + [rs_r], writes=["hB"])

    def mlp_front_b(ti):
        transposes8(hB, ["hB"], hTB2[ti % 2], ("hTB", ti % 2), "dve")

    def mlp_front_early():
        dead = ["hA", ("hTA", 0), ("hTA", 1)]
        S.pending["act"] = list(dead)
        S.pending["dve"] = list(dead)
        mlp_front_a(0)
        mlp_front_b(0)
        S.pending.pop("act", None)
        S.pending.pop("dve", None)
        yield

    EARLY_FRONT = NT >= 6
    bgs_a = [bg_s, bg_w]
    if EARLY_FRONT:
        bgs_a.append({"start": NT + 1, "fn": mlp_front_early, "drain_at": None})
    if PRECAST:
        bgs_a.append({"start": 5, "fn": precast_bg, "drain_at": NT - 3})
    run_steps(stepA, NT + 2, bgs=bgs_a)

    LD.flush()
    S.barrier(lambda e: e.memset(smalls[:, 61:62], 0.0))
    bg_w["gen"] = None
    S.op("sp", lambda e: e.dma_start(out=lng_bc[:, :], in_=clng_d.partition_broadcast(128)), writes=["lng"], dma="c6")
    S.op("sp", lambda e: e.dma_start(out=lnb_bc[:, :], in_=clnb_d.partition_broadcast(128)), writes=["lnb"], dma="c7")
    S.op("sp", lambda e: e.dma_start(out=gfin_bc[:, :], in_=nf_d.partition_broadcast(128)), writes=["gfin"], dma="c8")
    for cb in range(2):
        if PRECAST:
            S.op("sp", lambda e, cb=cb: e.dma_start(out=w_out[:, :, cb * 512:(cb + 1) * 512], in_=wscr2[:, :, cb * 512:(cb + 1) * 512]),
                 reads=[("wscr", 6 + cb, kt) for kt in range(8)], writes=[("wB_out", cb * 512, kt) for kt in range(8)],
                 dma="swo%d" % cb)
        else:
            for kp in range(8):
                LDB.add(("wB_out", cb * 512), lambda kp=kp, cb=cb: load_weight_chunk(
                    cwout_d, kp, cb * 512, 512, w_out, None, [], "wB_out", wstageB, "wstB", []))
    loaders["wB_in"] = LDB
    loaders["wB_out"] = LDB

    def mlp_ln_chain(ti):
        par = ti % 2
        samp = (ti == 0)
        vtp = vt[par]
        mv, mv_r = small(2)
        S.op("dve", lambda e: e.bn_aggr(out=mv, in_=bnst[par][:, 0:12]), reads=[("bnst", par, 0), ("bnst", par, 1)], writes=[mv_r])
        nmr, nmr_r = small()
        rsv, rsv_r = rstd_ops_pool(mv[:, 1:2], [mv_r], 1.0, 1)
        S.op("dve", lambda e: e.tensor_scalar(out=nmr, in0=mv[:, 0:1], scalar1=rsv, scalar2=-1.0, op0=ALU.mult, op1=ALU.mult),
             reads=[mv_r, rsv_r], writes=[nmr_r])
        for j in range(2):
            sl = slice(j * 512, (j + 1) * 512)
            S.op("act", lambda e, sl=sl: e.activation(out=vtp[:, sl], in_=vtp[:, sl], func=AF.Identity, scale=rsv, bias=nmr),
                 reads=[("vt", par, j), rsv_r, nmr_r], writes=[("vt", par, j)])
        for j in range(2):
            sl = slice(j * 512, (j + 1) * 512)
            S.op("dve", lambda e, sl=sl: e.tensor_tensor(out=vtp[:, sl], in0=vtp[:, sl], in1=lng_bc[:, sl], op=ALU.mult),
                 reads=[("vt", par, j), "lng"], writes=[("vt", par, j)])
            if samp:
                S.op("dve", lambda e, sl=sl: e.tensor_tensor(out=vtp[:, sl], in0=vtp[:, sl], in1=lnb_bc[:, sl], op=ALU.add),
                     reads=[("vt", par, j), "lnb"], writes=[("vt", par, j)])
                S.op("act", lambda e, sl=sl: e.activation(out=vn[:, sl], in_=vtp[:, sl], func=AF.Copy),
                     reads=[("vt", par, j)], writes=[("vn", j)])
            else:
                S.op("dve", lambda e, sl=sl: e.tensor_tensor(out=vn[:, sl], in0=vtp[:, sl], in1=lnb_bc[:, sl], op=ALU.add),
                     reads=[("vt", par, j), "lnb"], writes=[("vn", j)])
        if samp:
            S.op("sp", lambda e: e.dma_start(out=vs_d, in_=vtp[:, :]), reads=[("vt", par, 0), ("vt", par, 1)], dma="vs")


    def mlp_P(ti):
        par = ti % 2
        hTB = hTB2[par]
        hTB_r = ("hTB", par)
        first = (ti == 0)
        if first:
            for j in range(2):
                b = proj_chunk(hTB, hTB_r, w_in_B, "wB_in", j * 512, 512, 128)
                S.op("dve", lambda e, b=b, j=j: e.tensor_copy(out=ubuf[par][:, j * 512:(j + 1) * 512], in_=banks[b][:, 0:512]),
                     reads=[R(b)], writes=[("us", par, j)])
                pfree(b)
                yield
        for j in range(2):
            b = proj_chunk(hTB, hTB_r, w_in_B, "wB_in", D + j * 512, 512, 128)
            sl = slice(j * 512, (j + 1) * 512)
            S.op("act", lambda e, b=b, sl=sl: e.activation(out=vt[par][:, sl], in_=banks[b][:, 0:512], func=AF.Copy),
                 reads=[R(b)], writes=[("vt", par, j)])
            S.op("dve", lambda e, sl=sl, j=j: e.bn_stats(out=bnst[par][:, 6 * j:6 * j + 6], in_=vt[par][:, sl]),
                 reads=[("vt", par, j)], writes=[("bnst", par, j)])
            pfree(b)
            if j == 0 and ti + 1 < NT:
                mlp_front_a(ti + 1)
            if j == 1:
                mlp_ln_chain(ti)
            yield
        if ti + 1 < NT:
            mlp_front_b(ti + 1)
            yield
        gb = [proj_chunk(hTB, hTB_r, w_in_B, "wB_in", 2 * D + j * 512, 512, 128) for j in range(2)]
        for j in range(2):
            S.op("act", lambda e, b=gb[j], j=j: e.activation(out=sg1[par][:, j * 512:(j + 1) * 512], in_=banks[b][:, 0:512],
                                                            func=AF.Silu), reads=[R(gb[j])], writes=[("sg1", par, j)])
        pfree(*gb)
        yield
        for j in range(2):
            sl = slice(j * 512, (j + 1) * 512)
            if first:
                S.op("dve", lambda e, sl=sl: e.tensor_tensor(out=ubuf[par][:, sl], in0=ubuf[par][:, sl], in1=sg1[par][:, sl],
                                                             op=ALU.mult),
                     reads=[("us", par, j), ("sg1", par, j)], writes=[("us", par, j)])
            else:
                b = proj_chunk(hTB, hTB_r, w_in_B, "wB_in", j * 512, 512, 128)
                S.op("dve", lambda e, b=b, sl=sl: e.tensor_tensor(out=ubuf[par][:, sl], in0=banks[b][:, 0:512],
                                                                  in1=sg1[par][:, sl], op=ALU.mult),
                     reads=[R(b), ("sg1", par, j)], writes=[("us", par, j)])
                pfree(b)
            yield
        if first:
            LDB.flush()

    def mlp_C1(ti):
        par = ti % 2
        samp = (ti == 0)
        wsT = wsT_s if samp else wsT_p
        bsT = bsT_s if samp else bsT_p
        mb = [pbank(), pbank()]

        def fn(e):
            ins = None
            for g in range(4):
                ins = e.matmul(banks[mb[g // 2]][:, (g % 2) * 256:(g % 2) * 256 + 256], lhsT=wsT[:, g, :],
                               rhs=vn[:, hsl(g)], start=(g % 2 == 0), stop=True, skip_group_check=True)
            return ins
        S.op("pe", fn, reads=[("vn", 0), ("vn", 1)] + [("wsT_s" if samp else "wsT_p", g) for g in range(4)],
             writes=[R(mb[0]), R(mb[1])])
        for g in range(4):
            m_ap = banks[mb[g // 2]][:, (g % 2) * 256:(g % 2) * 256 + 256]
            S.op("dve", lambda e, g=g, m_ap=m_ap: e.scalar_tensor_tensor(
                out=zb[:, hsl(g)], in0=m_ap, scalar=bsT[:, g:g + 1], in1=ubuf[par][:, hsl(g)],
                op0=ALU.add, op1=ALU.mult),
                reads=[R(mb[g // 2]), "bsT_s", "vecT", ("us", par, g // 2)], writes=[("zb", g)])
        pfree(*mb)
        yield

    def mlp_C2(ti):
        transposes8(zb, [("zb", g) for g in range(4)], zT, "zT", "act")
        yield
        ssq2, ssq2_r = small(2)
        yo = yout[0]
        for j in range(2):
            b = proj_chunk(zT, "zT", w_out, "wB_out", j * 512, 512, 128)
            sl = slice(j * 512, (j + 1) * 512)
            S.op("dve", lambda e, b=b, sl=sl: e.tensor_tensor(out=x1all[:, ti, sl], in0=x1all[:, ti, sl], in1=banks[b][:, 0:512],
                                                              op=ALU.add), reads=[R(b), ("x1", ti, j)], writes=[("x1", ti, j)])
            S.op("act", lambda e, sl=sl, j=j: e.activation(out=yo[:, sl], in_=x1all[:, ti, sl], func=AF.Square,
                                                           accum_out=ssq2[:, j:j + 1]),
                 reads=[("x1", ti, j)], writes=[("yo", 0, j), (ssq2_r, j)])
            pfree(b)
            yield
        sst, sst_r = small()
        S.op("dve", lambda e: e.tensor_tensor(out=sst, in0=ssq2[:, 0:1], in1=ssq2[:, 1:2], op=ALU.add),
             reads=[(ssq2_r, 0), (ssq2_r, 1)], writes=[sst_r])
        rs, rs_r = rstd_ops_pool(sst, [sst_r], D, 1)
        for j in range(2):
            sl = slice(j * 512, (j + 1) * 512)
            S.op("dve", lambda e, sl=sl: e.scalar_tensor_tensor(out=yo[:, sl], in0=x1all[:, ti, sl], scalar=rs, in1=gfin_bc[:, sl],
                                                                op0=ALU.mult, op1=ALU.mult),
                 reads=[("x1", ti, j), rs_r, "gfin"], writes=[("yo", 0, j)])
        S.op("sp", lambda e: e.dma_start(out=y_d[ti * 128:(ti + 1) * 128, :], in_=yo[:, :]),
             reads=[("yo", 0, 0), ("yo", 0, 1)], dma="yout0")
        yield

    def stepB(s):
        gens = []
        t2 = s - 2
        if 0 <= t2 < NT:
            gens.append(mlp_C2(t2))
        t1 = s - 1
        if 0 <= t1 < NT:
            gens.append(mlp_C1(t1))
        if s < NT:
            gens.append(mlp_P(s))
        return gens

    if not EARLY_FRONT:
        mlp_front_a(0)
        mlp_front_b(0)
    run_steps(stepB, NT + 2)

    out_keys = [k for k in S.cnt if k.startswith("dma:yout") or k.startswith("dma:sbout") or k in ("dma:vs", "dma:stp")]
    S.final_wait("sp", out_keys)

    sem_keys = list(S.cnt.keys())
    sems = {k: es.enter_context(nc.semaphore("s_" + k.replace(":", "_"))) for k in sem_keys}
    block = es.enter_context(nc.Block())

    def emit(eng_obj, name):
        for waits, fn, sig in S.prog[name]:
            for k, v in waits:
                eng_obj.wait_ge(sems[k], v)
            if fn is not None:
                ins = fn(eng_obj)
                ins.then_inc(sems[sig[0]], sig[1])

    @block.sync
    def _(e):
        emit(e, "sp")

    @block.tensor
    def _(e):
        emit(e, "pe")

    @block.scalar
    def _(e):
        emit(e, "act")

    @block.vector
    def _(e):
        emit(e, "dve")

    @block.gpsimd
    def _(e):
        emit(e, "pool")

    es.close()
    return nc


def _consts():
    c = np.zeros((128, 1032), np.float32)
    s = np.arange(128)[:, None]
    t = np.arange(128)[None, :]
    c[:, 0:128] = np.eye(128, dtype=np.float32)
    c[:, 128:256] = np.where(s <= t, -1.0 / 16, 0.0)
    c[:, 256:384] = np.where(s > t, -1.0 / 16, 0.0)
    c[:, 384:512] = np.where(s <= t, 1.0, 0.0)
    same = (s // 8) == (t // 8)
    c[:, 512:640] = np.where(same & (s <= t), -1.0 / 16, 0.0)
    c[:, 640:768] = np.where(same & (s > t), -1.0 / 16, 0.0)
    c[:, 768:896] = np.where(same & (s <= t), 1.0, 0.0)
    bb = np.arange(8)[None, :]
    c[:, 896:904] = np.where(((s // 8) % 8) == bb, 1.0, 0.0)
    r = np.arange(8)[:, None]
    c[0:8, 904:1032] = np.where((t % 8) == r, 1.0, 0.0)
    mq = np.where(((t // 8) % 8) == np.arange(8)[:, None], 1.0, 0.0).astype(np.float32).reshape(1024)
    return c, mq


_NC_CACHE = {}


def kernel(x_prompt, x_sample, state_gla, norm_g, gla_w_in, gla_w_a_up, gla_b_a_up, gla_g_onorm,
           gla_w_out, cmlp_w_in, cmlp_ln_g, cmlp_ln_b, cmlp_w_spatial, cmlp_b_spatial, cmlp_w_out,
           norm_final):
    f = lambda a: np.ascontiguousarray(np.asarray(a, dtype=np.float32))
    x_prompt, x_sample, state_gla = f(x_prompt), f(x_sample), f(state_gla)
    Bp, T, _ = x_prompt.shape
    NTP = T // 128
    NT = NTP + 1
    if NTP not in _NC_CACHE:
        _NC_CACHE[NTP] = build(NTP)
    nc = _NC_CACHE[NTP]
    consts, mq = _consts()
    vecrows = np.concatenate([f(norm_g).reshape(16, 128), f(gla_g_onorm).reshape(2, 128),
                              f(cmlp_b_spatial).reshape(4, 128)], axis=0)
    shared = {
        "gla_w_in": f(gla_w_in)[0], "gla_w_a_up": f(gla_w_a_up)[0], "gla_b_a_up": f(gla_b_a_up).reshape(1, 512),
        "gla_w_out": f(gla_w_out)[0], "cmlp_w_in": f(cmlp_w_in)[0], "cmlp_ln_g": f(cmlp_ln_g).reshape(D),
        "cmlp_ln_b": f(cmlp_ln_b).reshape(D), "cmlp_w_spatial": f(cmlp_w_spatial)[0],
        "cmlp_w_out": f(cmlp_w_out)[0], "norm_final": f(norm_final).reshape(D),
        "vecrows": np.ascontiguousarray(vecrows), "consts": consts, "mq": mq,
    }
    in_maps = []
    for c in range(NCORES):
        xs = x_sample[16 * c:16 * (c + 1)].reshape(128, D)
        xc = np.concatenate([xs, x_prompt[c].reshape(NTP * 128, D)], axis=0)
        m = dict(shared)
        m["x"] = np.ascontiguousarray(xc)
        m["st"] = np.ascontiguousarray(state_gla[0, 16 * c:16 * (c + 1)])
        in_maps.append(m)
    res = run_bass_kernel_spmd(nc, in_maps, core_ids=list(range(NCORES)))
    outs = res.results
    y_prompt = np.stack([outs[c]["y"][128:].reshape(T, D) for c in range(NCORES)], axis=0)
    y_sample = np.concatenate([outs[c]["y"][:128].reshape(16, 8, D) for c in range(NCORES)], axis=0)
    st_p = np.stack([outs[c]["stp"] for c in range(NCORES)], axis=0)[None]
    st_s = np.concatenate([outs[c]["sts"] for c in range(NCORES)], axis=0)[None]
    v_s = np.concatenate([outs[c]["vs"].reshape(16, 8, D) for c in range(NCORES)], axis=0)[None]
    return (y_prompt.astype(np.float32), y_sample.astype(np.float32), st_p.astype(np.float32),
            st_s.astype(np.float32), v_s.astype(np.float32))
```

```python
import numpy as np
from contextlib import ExitStack
import concourse.bass as bass
import concourse.mybir as mybir
from concourse.bass_utils import run_bass_kernel_spmd

F32 = mybir.dt.float32
BF16 = mybir.dt.bfloat16
U8 = mybir.dt.uint8
AF = mybir.ActivationFunctionType
ALU = mybir.AluOpType

D = 1024
H = 4
DK = 128
DV = 256
GW = 3088
NCORES = 8
DRAIN_AT = None
EPS = 1e-6
ENGS = ("pe", "act", "dve", "pool", "sp")


class Sched:
    def __init__(self):
        self.prog = {e: [] for e in ENGS}
        self.cnt = {}
        self.res = {}
        self.seen = {e: {} for e in ENGS}
        self.pending = {}
        self.snap = {}

    def _collect(self, eng, reads, writes, after):
        waits = {}

        def need(dep):
            if dep is None:
                return
            k, v = dep
            if eng == "pe" and k == "pe":
                return
            if waits.get(k, 0) < v:
                waits[k] = v

        for r in reads:
            st = self.res.get(r)
            if st:
                need(st["w"])
        for w in list(writes) + list(after):
            st = self.res.get(w)
            if st:
                need(st["w"])
                for k, v in st["r"].items():
                    need((k, v))
        out = []
        seen = self.seen[eng]
        for k, v in sorted(waits.items(), key=lambda kv: -len(self.snap.get(kv, ()))):
            if seen.get(k, 0) >= v:
                continue
            seen[k] = v
            out.append((k, v))
            for k2, v2 in self.snap.get((k, v), {}).items():
                if seen.get(k2, 0) < v2:
                    seen[k2] = v2
        return out

    def op(self, eng, fn, reads=(), writes=(), after=(), dma=None):
        after = list(after) + self.pending.pop(eng, [])
        waits = self._collect(eng, reads, writes, after)
        if dma is not None:
            key = "dma:" + dma
            inc = 16
        else:
            key = eng
            inc = 1
        val = self.cnt.get(key, 0) + inc
        self.cnt[key] = val
        self.snap[(key, val)] = dict(self.seen[eng])
        self.prog[eng].append((waits, fn, (key, inc)))
        for r in reads:
            st = self.res.setdefault(r, {"w": None, "r": {}})
            if st["r"].get(key, 0) < val:
                st["r"][key] = val
        for w in writes:
            self.res[w] = {"w": (key, val), "r": {}}
        return (key, val)

    def barrier(self, pool_fn):
        keys = [k for k in self.cnt if k.startswith("dma:")] + [k for k in self.cnt if not k.startswith("dma:") and k != "pool"]
        waits = []
        for k in keys:
            v = self.cnt[k]
            if self.seen["pool"].get(k, 0) >= v:
                continue
            self.seen["pool"][k] = v
            waits.append((k, v))
        val = self.cnt.get("pool", 0) + 1
        self.cnt["pool"] = val
        self.prog["pool"].append((waits, pool_fn, ("pool", 1)))
        for e in ENGS:
            if e == "pool":
                continue
            for k, v in self.cnt.items():
                self.seen[e][k] = max(self.seen[e].get(k, 0), v if k != "pool" else 0)
            self.seen[e]["pool"] = val
            self.prog[e].append(([("pool", val)], None, None))

    def final_wait(self, eng, keys):
        waits = [(k, self.cnt[k]) for k in keys if k in self.cnt]
        self.prog[eng].append((waits, None, None))


class Carver:
    def __init__(self, handle, base, limit):
        self.h = handle
        self.off = base
        self.limit = limit

    def alloc(self, free_shape, dtype):
        esz = 4 if dtype == F32 else 2
        n = 1
        for s in free_shape:
            n *= s
        nbytes = n * esz
        off = (self.off + 63) // 64 * 64
        assert off + nbytes <= self.limit, (off, nbytes, self.limit)
        self.off = off + nbytes
        v = self.h[:, off:off + nbytes].bitcast(dtype)
        if len(free_shape) == 2:
            v = v.rearrange("p (a b) -> p a b", a=free_shape[0])
        elif len(free_shape) == 3:
            v = v.rearrange("p (a b c) -> p a b c", a=free_shape[0], b=free_shape[1])
        return v


class Loader:
    def __init__(self):
        self.items = []
        self.pos = 0

    def add(self, key, thunk):
        self.items.append((key, thunk))

    def ensure(self, key):
        last = None
        for i in range(self.pos, len(self.items)):
            if self.items[i][0] == key:
                last = i
        if last is None:
            return
        while self.pos <= last:
            self.items[self.pos][1]()
            self.pos += 1

    def step(self):
        if self.pos < len(self.items):
            self.items[self.pos][1]()
            self.pos += 1
            return True
        return False

    def flush(self):
        while self.step():
            pass


def _double_first(g):
    try:
        next(g)
        next(g)
    except StopIteration:
        return
    yield
    for _ in g:
        yield


def _roundrobin(gens):
    gens = list(gens)
    while gens:
        nxt = []
        for g in gens:
            try:
                next(g)
                nxt.append(g)
            except StopIteration:
                pass
        gens = nxt


def build(NTP=16, SBUF_KB=207):
    NT = NTP + 1
    nc = bass.Bass("TRN2", target_bir_lowering=False)

    def din(name, shape):
        return nc.dram_tensor(name, shape, F32, kind="ExternalInput").ap()

    def dout(name, shape):
        return nc.dram_tensor(name, shape, F32, kind="ExternalOutput").ap()

    x_d = din("x", [NT * 128, D])
    st_d = din("st", [16, H, DK, DV])
    gwin_d = din("gla_w_in", [D, GW])
    gwau_d = din("gla_w_a_up", [16, 512])
    gbau_d = din("gla_b_a_up", [1, 512])
    gwout_d = din("gla_w_out", [D, D])
    cwin_d = din("cmlp_w_in", [D, 3 * D])
    clng_d = din("cmlp_ln_g", [D])
    clnb_d = din("cmlp_ln_b", [D])
    cws_d = din("cmlp_w_spatial", [4, 128, 128])
    cwout_d = din("cmlp_w_out", [D, D])
    nf_d = din("norm_final", [D])
    vec_d = din("vecrows", [22, 128])
    const_d = din("consts", [128, 1032])
    mq_d = din("mq", [1024])

    wscr = nc.dram_tensor("wscr", [128, 8 * 3 * D], BF16, kind="Internal").ap().rearrange("p (k n) -> p k n", k=8)
    wscr2 = nc.dram_tensor("wscr2", [128, 8 * D], BF16, kind="Internal").ap().rearrange("p (k n) -> p k n", k=8)
    y_d = dout("y", [NT * 128, D])
    stp_d = dout("stp", [H, DK, DV])
    sts_d = dout("sts", [16, H, DK, DV])
    vs_d = dout("vs", [128, D])

    S = Sched()
    es = ExitStack()
    TOTAL = 207360
    main = es.enter_context(nc.sbuf_tensor("main", [128, TOTAL], U8))
    banks = [es.enter_context(nc.psum_tensor("ps%d" % i, [128, 512], F32)) for i in range(8)]
    banks_bf = [b.bitcast(BF16) for b in banks]

    pa = Carver(main, 0, TOTAL)
    w_in_flat = pa.alloc([8 * GW], BF16)
    w_out = pa.alloc([8, D], BF16)
    x1all = pa.alloc([NT, D], F32)
    cst = pa.alloc([1032], F32)
    ident_bf = pa.alloc([128], BF16)
    vecT = pa.alloc([24], F32)
    ngq = pa.alloc([8], F32)
    smalls = pa.alloc([64], F32)
    waug = pa.alloc([512], BF16)
    aT = [pa.alloc([128], BF16) for _ in range(2)]
    wsT_p = pa.alloc([4, 128], BF16)
    wsT_s = pa.alloc([4, 128], BF16)
    bsT_s = pa.alloc([4], F32)
    overlay_base = pa.off

    ident = cst[:, 0:128]
    triN_p, UN_p, mask_p = cst[:, 128:256], cst[:, 256:384], cst[:, 384:512]
    triN_s, UN_s, mask_s = cst[:, 512:640], cst[:, 640:768], cst[:, 768:896]
    bmask8 = cst[:, 896:904]
    R8 = cst[:, 904:1032]
    w_in_A = w_in_flat.rearrange("p (k n) -> p k n", k=8)
    w_in_B = w_in_A[:, :, 0:3 * D]

    ALIAS0 = 6
    if NT >= 17:
        _st = [0]
        xflat = x1all[:, ALIAS0:NT, :].rearrange("p a b -> p (a b)")
        xcap = (NT - ALIAS0) * D

        def sc_alloc(free_shape, dtype):
            esz = 4 if dtype == F32 else 2
            n = 1
            for s in free_shape:
                n *= s
            nwords = (n * esz + 3) // 4
            w0 = (_st[0] + 15) // 16 * 16
            assert w0 + nwords <= xcap, (w0, nwords, xcap)
            _st[0] = w0 + nwords
            v = xflat[:, w0:w0 + nwords].bitcast(dtype)
            if len(free_shape) == 2:
                v = v.rearrange("p (a b) -> p a b", a=free_shape[0])
            elif len(free_shape) == 3:
                v = v.rearrange("p (a b c) -> p a b c", a=free_shape[0], b=free_shape[1])
            return v
        SFENCE = True
        A = Carver(main, overlay_base, TOTAL)
    else:
        XA = Carver(main, overlay_base, overlay_base + 50 * 1024)
        sc_alloc = XA.alloc
        SFENCE = False
        A = Carver(main, overlay_base + 50 * 1024, TOTAL)
    NSB = 4
    stage_blk = sc_alloc([4, 1024], F32)
    wstage = [stage_blk[:, i // 2, (i % 2) * 512:(i % 2) * 512 + 512] for i in range(8)]
    Sb = [stage_blk[:, i, :].rearrange("p (h v) -> p h v", h=4) for i in range(NSB)]
    Sb_bf = [sc_alloc([4, DV], BF16) for _ in range(2)]
    qblk = sc_alloc([2048], F32)
    qm8 = qblk.bitcast(BF16).rearrange("p (a b c) -> p a b c", a=4, b=8)
    kdm8 = sc_alloc([8, 512], BF16)
    mq = sc_alloc([8, 128], F32)
    vbf_s = sc_alloc([D], BF16)
    sg_s = sc_alloc([D], BF16)
    misc = qblk[:, 0:1280]
    wsall = misc[:, 0:512].rearrange("p (g s) -> p g s", g=4)
    T1 = misc[:, 512:640]
    vecrows = misc[:, 640:768]
    waug_st = misc[:, 768:1280]

    hA = A.alloc([D], BF16)
    hTA2 = [A.alloc([8, 128], BF16) for _ in range(2)]
    qk = [A.alloc([D], BF16) for _ in range(2)]
    vbf = [A.alloc([D], BF16) for _ in range(2)]
    sg = [A.alloc([D], BF16) for _ in range(2)]
    spb = A.alloc([512], F32)
    ecT = A.alloc([4, 128], F32)
    encT = A.alloc([4, 128], F32)
    eU = A.alloc([512], F32)
    kdec = A.alloc([512], BF16)
    qTt = A.alloc([4, 128], BF16)
    kTt = A.alloc([4, 128], BF16)
    scT = A.alloc([4, 128], BF16)
    og = A.alloc([D], BF16)
    ogT = A.alloc([8, 128], BF16)
    Sst = A.alloc([4, DV], F32)
    Sbf = A.alloc([4, DV], BF16)
    elast = A.alloc([4, 16], F32)
    sp_hi = A.alloc([512], BF16)
    sp_lo = A.alloc([512], BF16)
    cst_bf = A.alloc([4, 128], BF16)
    og_s = A.alloc([D], BF16)
    ogT_s = A.alloc([8, 128], BF16)

    B = Carver(main, overlay_base, TOTAL)
    NWB = 3
    TOPC = Carver(main, TOTAL - NWB * 4096, TOTAL)
    wstageB = [TOPC.alloc([512], F32) for _ in range(2 * NWB)]
    B.limit = TOTAL - NWB * 4096
    A.limit = TOTAL - NWB * 4096
    hB = B.alloc([D], BF16)
    _hb1 = B.alloc([8, 128], BF16)
    _hb0 = B.alloc([8, 128], BF16)
    hTB2 = [_hb0, _hb1]
    lng_bc = B.alloc([D], F32)
    lnb_bc = B.alloc([D], F32)
    gfin_bc = B.alloc([D], F32)
    ubuf = [B.alloc([D], F32) for _ in range(2)]
    sg1 = [B.alloc([D], BF16) for _ in range(2)]
    vt = [B.alloc([D], F32) for _ in range(2)]
    vn = B.alloc([D], BF16)
    zb = B.alloc([D], BF16)
    zT = B.alloc([8, 128], BF16)
    yout = [B.alloc([D], F32) for _ in range(1)]
    bnst = [B.alloc([12], F32) for _ in range(2)]
    print("SBUF bytes: persistent %d, A end %d, B end %d, total %d" % (overlay_base, A.off, B.off, TOTAL))

    sm_ctr = [0]

    def small(n=1):
        c = sm_ctr[0]
        if c + n > 48:
            c = 0
        sm_ctr[0] = c + n
        return smalls[:, c:c + n], ("sm", c, n)

    ONE_AP = smalls[:, 62:63]
    EPS_AP = smalls[:, 63:64]

    from collections import deque
    free_q = deque(range(8))

    def pbank():
        assert free_q, "out of PSUM banks"
        return free_q.popleft()

    def pfree(*bs):
        for b in bs:
            assert b not in free_q
            free_q.append(b)

    def R(b):
        return ("ps", b)

    cast_rr = [0]

    def cast_op(out, in_, scale, reads, writes, after=()):
        e = ("act", "dve")[cast_rr[0] % 2]
        cast_rr[0] += 1
        if e == "act":
            if scale is None:
                fn = lambda en: en.activation(out=out, in_=in_, func=AF.Copy)
            else:
                fn = lambda en: en.activation(out=out, in_=in_, func=AF.Identity, scale=scale)
        else:
            if scale is None:
                fn = lambda en: en.tensor_copy(out=out, in_=in_)
            else:
                fn = lambda en: en.tensor_scalar(out=out, in0=in_, scalar1=scale, scalar2=None, op0=ALU.mult)
        S.op(e, fn, reads=reads, writes=writes, after=after)

    fence_l = ["SFENCE"] if SFENCE else []

    def load_x(ti):
        wr = [("x1", ti, 0), ("x1", ti, 1)]
        if SFENCE and ti >= ALIAS0:
            wr = wr + ["SFENCE"]
        S.op("sp", lambda e: e.dma_start(out=x1all[:, ti, :], in_=x_d[ti * 128:(ti + 1) * 128, :]),
             writes=wr, dma="x%d" % ti)

    S.op("sp", lambda e: e.dma_start(out=cst[:, 0:128], in_=const_d[:, 0:128]), writes=["cst_id"], dma="c0a")
    load_x(0)
    S.op("sp", lambda e: e.dma_start(out=vecrows[0:22, :], in_=vec_d), reads=fence_l, writes=["vecrows"], dma="c1")
    S.op("sp", lambda e: e.dma_start(out=waug_st[0:16, :], in_=gwau_d), reads=fence_l, writes=["waug_st0"], dma="c2")
    S.op("sp", lambda e: e.dma_start(out=waug_st[16:17, :], in_=gbau_d), reads=fence_l, writes=["waug_st1"], dma="c3")
    S.op("sp", lambda e: e.dma_start(out=wsall[:, :, :], in_=cws_d.rearrange("g t s -> t g s")),
         reads=fence_l, writes=["wsall"], dma="c4")
    S.op("sp", lambda e: e.dma_start(out=cst[:, 128:1032], in_=const_d[:, 128:1032]), writes=["cst"], dma="c0")

    S.op("pool", lambda e: e.memset(EPS_AP, EPS), writes=["epsc"])
    S.op("pool", lambda e: e.memset(ONE_AP, 1.0), writes=["onec"])
    S.op("dve", lambda e: e.tensor_copy(out=ident_bf[:, :], in_=ident), reads=["cst_id"], writes=["ident_bf"])
    S.op("dve", lambda e: e.tensor_copy(out=cst_bf[:, 0:2, :], in_=cst[:, 128:384].rearrange("p (a b) -> p a b", a=2)),
         reads=["cst"], writes=["cst_bf0"])
    S.op("dve", lambda e: e.tensor_copy(out=cst_bf[:, 2:4, :], in_=cst[:, 512:768].rearrange("p (a b) -> p a b", a=2)),
         reads=["cst"], writes=["cst_bf1"])
    S.op("dve", lambda e: e.tensor_copy(out=waug[0:17, :], in_=waug_st[0:17, :]),
         reads=["waug_st0", "waug_st1"] + fence_l, writes=["waug"])
    for i in range(2):
        S.op("pool", lambda e, i=i: e.memset(aT[i][0:17, :], 1.0), writes=[("aT", i)])
    b = pbank()
    S.op("pe", lambda e, b=b: e.transpose(out=banks[b][:, 0:22], in_=vecrows[0:22, :], identity=cst[0:22, 0:22]),
         reads=["vecrows", "cst_id"] + fence_l, writes=[R(b)])
    S.op("dve", lambda e, b=b: e.tensor_copy(out=vecT[:, 0:22], in_=banks[b][:, 0:22]), reads=[R(b)], writes=["vecT"])
    pfree(b)
    S.op("dve", lambda e: e.tensor_scalar(out=ngq[:, 0:8], in0=vecT[:, 0:8], scalar1=float(DK ** -0.5),
                                          scalar2=None, op0=ALU.mult), reads=["vecT"], writes=["ngq"])
    bsT_p = vecT[:, 18:22]

    wst_ctr = [0]

    def load_weight_chunk(src_d, kt, c0, ncols, dst3, scale8, scale_res, resname, stages, sname, fence, after=()):
        i = wst_ctr[0] % len(stages)
        n = wst_ctr[0]
        wst_ctr[0] += 1
        stg = stages[i]
        srcv = src_d.rearrange("(k p) n -> p k n", p=128)[:, kt, c0:c0 + ncols]
        S.op("sp", lambda e: e.dma_start(out=stg[:, 0:ncols], in_=srcv),
             reads=fence, writes=[(sname, i)], dma="%s%d" % (sname, i))
        wr = [(resname, c0 // 512 * 512, kt)]
        rd = [(sname, i)] + scale_res + fence
        dst = dst3[:, kt, c0:c0 + ncols]
        if n % 2 == 0:
            if scale8 is None:
                S.op("dve", lambda e: e.tensor_copy(out=dst, in_=stg[:, 0:ncols]), reads=rd, writes=wr, after=after)
            else:
                S.op("dve", lambda e: e.tensor_tensor(out=dst, in0=stg[:, 0:ncols],
                                                      in1=scale8[:, kt:kt + 1].to_broadcast([128, ncols]),
                                                      op=ALU.mult), reads=rd, writes=wr, after=after)
        else:
            if scale8 is None:
                S.op("act", lambda e: e.activation(out=dst, in_=stg[:, 0:ncols], func=AF.Copy), reads=rd, writes=wr, after=after)
            else:
                S.op("act", lambda e: e.activation(out=dst, in_=stg[:, 0:ncols], func=AF.Identity,
                                                   scale=scale8[:, kt:kt + 1]), reads=rd, writes=wr, after=after)

    def load_alow_chunk():
        i = wst_ctr[0] % 8
        wst_ctr[0] += 1
        stg = wstage[i][:, 0:128].rearrange("p (k n) -> p k n", k=8)
        srcv = gwin_d.rearrange("(k p) n -> p k n", p=128)[:, :, 3072:3088]
        S.op("sp", lambda e: e.dma_start(out=stg, in_=srcv), reads=fence_l, writes=[("wstA", i)], dma="wstA%d" % i)
        S.op("dve", lambda e: e.tensor_tensor(out=w_in_A[:, :, 3072:3088], in0=stg,
                                              in1=vecT[:, 0:8].unsqueeze(2).to_broadcast([128, 8, 16]), op=ALU.mult),
             reads=[("wstA", i), "vecT"] + fence_l, writes=[("wA_in", 3072, kp) for kp in range(8)])

    gon8 = smalls[:, 48:56]
    S.op("dve", lambda e: e.tensor_copy(out=gon8.rearrange("p (a b) -> p a b", b=2),
                                        in_=vecT[:, 16:18].unsqueeze(1).to_broadcast([128, 4, 2])),
         reads=["vecT"], writes=["gon8"])
    LD = Loader()
    loaders = {"wA_in": LD, "wA_out": LD}
    for cb in range(6):
        for kp in range(8):
            LD.add(("wA_in", cb * 512), lambda kp=kp, cb=cb: load_weight_chunk(
                gwin_d, kp, cb * 512, 512, w_in_A, ngq[:, 0:8] if cb == 0 else vecT[:, 0:8], ["ngq", "vecT"],
                "wA_in", wstage, "wstA", fence_l))
        if cb == 1:
            LD.add(("wA_in", 3072), load_alow_chunk)
    for cb in range(2):
        for kp in range(8):
            LD.add(("wA_out", cb * 512), lambda kp=kp, cb=cb: load_weight_chunk(
                gwout_d, kp, cb * 512, 512, w_out, gon8, ["gon8"], "wA_out", wstage, "wstA", fence_l))

    def wres(name, c0, c1, cw=128):
        return [(name, cb, kp) for cb in range(c0 // 512 * 512, c1, 512) for kp in range(8)]

    LDB = Loader()
    PRECAST = NT >= 12

    def swap_block(cb):
        S.op("sp", lambda e: e.dma_start(out=w_in_B[:, :, cb * 512:(cb + 1) * 512], in_=wscr[:, :, cb * 512:(cb + 1) * 512]),
             reads=[("wscr", cb, kt) for kt in range(8)], writes=[("wB_in", cb * 512, kt) for kt in range(8)],
             after=wres("wA_in", cb * 512, cb * 512 + 512), dma="swp%d" % cb)

    for cb in range(6):
        if PRECAST:
            LDB.add(("wB_in", cb * 512), lambda cb=cb: swap_block(cb))
        else:
            for kp in range(8):
                LDB.add(("wB_in", cb * 512), lambda kp=kp, cb=cb: load_weight_chunk(
                    cwin_d, kp, cb * 512, 512, w_in_B, vecT[:, 8:16], ["vecT"], "wB_in", wstageB, "wstB", [],
                    after=wres("wA_in", cb * 512, cb * 512 + 512)))

    def precast_bg():
        chunks = [(cb, kt) for cb in range(8) for kt in range(8)]
        NCH = len(chunks)

        def stage(n):
            i = n % 4
            return i, wstageB[i], wstageB[4 + i // 2].bitcast(BF16)[:, (i % 2) * 512:(i % 2) * 512 + 512]

        for r in range(NCH + 5):
            if r < NCH:
                cb, kt = chunks[r]
                i, stg, ost = stage(r)
                if cb < 6:
                    srcv = cwin_d.rearrange("(k p) n -> p k n", p=128)[:, kt, cb * 512:(cb + 1) * 512]
                else:
                    srcv = cwout_d.rearrange("(k p) n -> p k n", p=128)[:, kt, (cb - 6) * 512:(cb - 5) * 512]
                S.op("sp", lambda e, stg=stg, srcv=srcv: e.dma_start(out=stg[:, :], in_=srcv),
                     writes=[("pci", i)], dma="pci%d" % i)
            n = r - 3
            if 0 <= n < NCH:
                cb, kt = chunks[n]
                i, stg, ost = stage(n)
                sc = vecT[:, 8 + kt:9 + kt]
                if cb >= 6:
                    if n % 2 == 0:
                        S.op("dve", lambda e, stg=stg, ost=ost: e.tensor_copy(out=ost, in_=stg[:, :]),
                             reads=[("pci", i)], writes=[("pco", i)])
                    else:
                        S.op("act", lambda e, stg=stg, ost=ost: e.activation(out=ost, in_=stg[:, :], func=AF.Copy),
                             reads=[("pci", i)], writes=[("pco", i)])
                elif n % 2 == 0:
                    S.op("dve", lambda e, stg=stg, ost=ost, sc=sc: e.tensor_tensor(
                        out=ost, in0=stg[:, :], in1=sc.to_broadcast([128, 512]), op=ALU.mult),
                        reads=[("pci", i), "vecT"], writes=[("pco", i)])
                else:
                    S.op("act", lambda e, stg=stg, ost=ost, sc=sc: e.activation(out=ost, in_=stg[:, :], func=AF.Identity, scale=sc),
                         reads=[("pci", i), "vecT"], writes=[("pco", i)])
            n = r - 5
            if 0 <= n < NCH:
                cb, kt = chunks[n]
                i, stg, ost = stage(n)
                dstv = wscr[:, kt, cb * 512:(cb + 1) * 512] if cb < 6 else wscr2[:, kt, (cb - 6) * 512:(cb - 5) * 512]
                S.op("sp", lambda e, ost=ost, dstv=dstv: e.dma_start(out=dstv, in_=ost),
                     reads=[("pco", i)], writes=[("wscr", cb, kt)], dma="pco%d" % i)
            yield

    LD.ensure(("wA_in", 0))
    if NT > 1:
        load_x(1)

    for g in range(4):
        b = pbank()
        S.op("pe", lambda e, b=b, g=g: e.transpose(out=banks[b][:, 0:128], in_=wsall[:, g, :], identity=ident),
             reads=["wsall", "cst", "cst_id"] + fence_l, writes=[R(b)])
        S.op("dve", lambda e, b=b, g=g: e.tensor_tensor(out=wsT_p[:, g, :], in0=banks[b][:, 0:128], in1=mask_p,
                                                         op=ALU.mult), reads=[R(b), "cst"], writes=[("wsT_p", g)])
        pfree(b)
        b = pbank()
        S.op("pe", lambda e, b=b, g=g: e.matmul(banks[b][0:8, 0:128], lhsT=wsall[0:8, g, 0:8], rhs=R8[0:8, :],
                                                start=True, stop=True),
             reads=["wsall", "cst"] + fence_l, writes=[R(b)])
        S.op("dve", lambda e, b=b: e.tensor_copy(out=T1[0:8, :], in_=banks[b][0:8, 0:128]), reads=[R(b)] + fence_l, writes=["T1"])
        pfree(b)
        b = pbank()
        S.op("pe", lambda e, b=b: e.matmul(banks[b][:, 0:128], lhsT=R8[0:8, :], rhs=T1[0:8, :], start=True, stop=True),
             reads=["T1", "cst"] + fence_l, writes=[R(b)])
        S.op("dve", lambda e, b=b, g=g: e.tensor_tensor(out=wsT_s[:, g, :], in0=banks[b][:, 0:128], in1=mask_s,
                                                         op=ALU.mult), reads=[R(b), "cst"], writes=[("wsT_s", g)])
        pfree(b)
    b = pbank()
    S.op("pe", lambda e, b=b: e.matmul(banks[b][:, 0:4], lhsT=R8[0:8, :], rhs=vecT[0:8, 18:22], start=True, stop=True),
         reads=["vecT", "cst"], writes=[R(b)])
    S.op("dve", lambda e, b=b: e.tensor_copy(out=bsT_s[:, 0:4], in_=banks[b][:, 0:4]), reads=[R(b)], writes=["bsT_s"])
    pfree(b)

    S.op("sp", lambda e: e.dma_start(out=mq.rearrange("p a b -> p (a b)"), in_=mq_d.partition_broadcast(128)),
         reads=fence_l, writes=["mq"], dma="c5")

    def rstd_ops(ssq_ap, ssq_res, n_elems, width):
        lnv, lnv_r = small(width)
        rs, rs_r = small(width)
        S.op("act", lambda e: e.activation(out=lnv, in_=ssq_ap, func=AF.Ln, scale=1.0 / n_elems, bias=EPS_AP),
             reads=list(ssq_res) + ["epsc"], writes=[lnv_r])
        S.op("act", lambda e: e.activation(out=rs, in_=lnv, func=AF.Exp, scale=-0.5), reads=[lnv_r], writes=[rs_r])
        return rs, rs_r

    NEGHALF = smalls[:, 57:61]
    S.op("pool", lambda e: e.memset(NEGHALF, -0.5), writes=["neghalf"])

    def rstd_ops_pool(ssq_ap, ssq_res, n_elems, width):
        tmp, tmp_r = small(width)
        rs, rs_r = small(width)
        S.op("pool", lambda e: e.tensor_scalar(out=tmp, in0=ssq_ap, scalar1=1.0 / n_elems, scalar2=EPS, op0=ALU.mult, op1=ALU.add),
             reads=list(ssq_res), writes=[tmp_r])
        S.op("pool", lambda e: e.tensor_tensor(out=rs, in0=tmp, in1=NEGHALF[:, 0:width], op=ALU.pow),
             reads=[tmp_r, "neghalf"], writes=[rs_r])
        return rs, rs_r

    def transposes8(src, src_res, dstT, dst_res, evac_eng):
        b = pbank()
        pv = banks_bf[b]

        def fn(e):
            ins = None
            for kt in range(8):
                ins = e.transpose(out=pv[:, kt * 128:(kt + 1) * 128], in_=src[:, kt * 128:(kt + 1) * 128],
                                  identity=ident_bf[:, :])
            return ins
        S.op("pe", fn, reads=list(src_res) + ["ident_bf"], writes=[R(b)])
        dflat = dstT.rearrange("p a b -> p (a b)")
        if evac_eng == "act":
            S.op("act", lambda e: e.activation(out=dflat, in_=pv[:, 0:1024], func=AF.Copy), reads=[R(b)], writes=[dst_res])
        else:
            S.op(evac_eng, lambda e: e.tensor_copy(out=dflat, in_=pv[:, 0:1024]), reads=[R(b)], writes=[dst_res])
        pfree(b)

    def proj_chunk(hT, hT_res, w3, wname, c0, ncols, cw=256, last_tile=False):
        loaders[wname].ensure((wname, c0 // 512 * 512))
        for _ in range(2 if (PRECAST and wname.startswith("wB")) else 8):
            loaders[wname].step()
        b = pbank()

        def fn(e):
            ins = None
            for kt in range(8):
                ins = e.matmul(banks[b][:, 0:ncols], lhsT=hT[:, kt, :], rhs=w3[:, kt, c0:c0 + ncols],
                               start=(kt == 0), stop=(kt == 7))
            return ins
        S.op("pe", fn, reads=[hT_res] + wres(wname, c0, c0 + ncols, cw), writes=[R(b)])
        if wname == "wA_in" and last_tile:
            swap_state["allowed"] = c0 // 512
        return b
    swap_state = {"allowed": -1}

    def hsl(h):
        return slice(h * 256, (h + 1) * 256)

    def gla_front_a(ti):
        xi = x1all[:, ti, :]
        xr = [("x1", ti, 0), ("x1", ti, 1)]
        ssq, ssq_r = small()
        S.op("act", lambda e: e.activation(out=hA[:, :], in_=xi, func=AF.Square, accum_out=ssq),
             reads=xr, writes=["hA", ssq_r])
        rs, rs_r = rstd_ops(ssq, [ssq_r], D, 1)
        S.op("act", lambda e: e.activation(out=hA[:, :], in_=xi, func=AF.Identity, scale=rs),
             reads=xr + [rs_r], writes=["hA"])

    def gla_front_b(ti):
        transposes8(hA, ["hA"], hTA2[ti % 2], ("hTA", ti % 2), "dve")

    def gla_C1_front(ti):
        par = ti % 2
        b = pbank()
        S.op("pe", lambda e, b=b: e.matmul(banks[b][:, 0:512], lhsT=aT[par][0:17, :], rhs=waug[0:17, :], start=True, stop=True),
             reads=[("aT", par), "waug"], writes=[R(b)])
        S.op("act", lambda e, b=b: e.activation(out=spb[:, :], in_=banks[b][:, 0:512], func=AF.Exp, scale=-1.0),
             reads=[R(b)], writes=["spb"])
        pfree(b)
        S.op("act", lambda e: e.activation(out=spb[:, :], in_=spb[:, :], func=AF.Ln, bias=ONE_AP), reads=["spb", "onec"], writes=["spb"])
        S.op("dve", lambda e: e.tensor_copy(out=sp_hi[:, :], in_=spb[:, :]), reads=["spb"], writes=["sp_hi"])
        S.op("dve", lambda e: e.tensor_tensor(out=sp_lo[:, :], in0=spb[:, :], in1=sp_hi[:, :], op=ALU.subtract),
             reads=["spb", "sp_hi"], writes=["sp_lo"])

    PAIR_LAST = False

    def gla_P(ti, fronts=True, c1front=True):
        par = ti % 2
        samp = (ti == 0)
        lt = (ti == NT - 1)
        sf = fence_l if samp else []
        v_dst, v_res = (vbf_s, "vbf_s") if samp else (vbf[par], ("vbf", par))
        g_dst, g_res = (sg_s, "sg_s") if samp else (sg[par], ("sg", par))
        if lt and PAIR_LAST:
            v_dst, v_res, g_dst, g_res = og_s, "vbf3", ogT_s.rearrange("p a b -> p (a b)"), "sg3"
        hTA = hTA2[par]
        hTA_r = ("hTA", par)
        b = proj_chunk(hTA, hTA_r, w_in_A, "wA_in", 0, 512, last_tile=lt)
        S.op("dve", lambda e, b=b: e.tensor_copy(out=qk[par][:, 0:512], in_=banks[b][:, 0:512]),
             reads=[R(b)], writes=[("qk", par, 0)])
        pfree(b)
        if fronts and ti + 1 < NT:
            gla_front_a(ti + 1)
        yield
        b = proj_chunk(hTA, hTA_r, w_in_A, "wA_in", 512, 512, last_tile=lt)
        S.op("dve", lambda e, b=b: e.tensor_copy(out=qk[par][:, 512:1024], in_=banks[b][:, 0:512]),
             reads=[R(b)], writes=[("qk", par, 1)])
        pfree(b)
        yield
        LD.ensure(("wA_in", 3072))
        b = pbank()

        def fn(e, b=b):
            ins = None
            for kt in range(8):
                ins = e.matmul(banks[b][0:16, 0:128], lhsT=w_in_A[:, kt, 3072:3088], rhs=hTA[:, kt, :],
                               start=(kt == 0), stop=(kt == 7))
            return ins
        S.op("pe", fn, reads=[hTA_r] + wres("wA_in", 3072, 3088), writes=[R(b)])
        S.op("dve", lambda e, b=b: e.tensor_copy(out=aT[par][0:16, :], in_=banks[b][0:16, 0:128]),
             reads=[R(b)], writes=[("aT", par)])
        pfree(b)
        yield
        if fronts and ti + 1 < NT:
            gla_front_b(ti + 1)
            yield
        if c1front:
            gla_C1_front(ti)
            yield
        for j in range(2):
            b = proj_chunk(hTA, hTA_r, w_in_A, "wA_in", 1024 + j * 512, 512, last_tile=lt)
            S.op("dve", lambda e, b=b, j=j: e.tensor_copy(out=v_dst[:, j * 512:(j + 1) * 512], in_=banks[b][:, 0:512]),
                 reads=[R(b)] + sf, writes=[(v_res, j)])
            pfree(b)
            yield
        gb = [proj_chunk(hTA, hTA_r, w_in_A, "wA_in", 2048 + j * 512, 512, last_tile=lt) for j in range(2)]
        for j in range(2):
            S.op("act", lambda e, b=gb[j], j=j: e.activation(out=g_dst[:, j * 512:(j + 1) * 512], in_=banks[b][:, 0:512],
                                                            func=AF.Silu), reads=[R(gb[j])] + sf, writes=[(g_res, j)])
        pfree(*gb)
        S.op("act", lambda e: e.activation(out=smalls[:, 56:57], in_=ONE_AP, func=AF.Exp), reads=["onec"], writes=["dummy"])
        yield

    def gla_C1(ti):
        par = ti % 2
        samp = (ti == 0)
        msk = mask_s if samp else mask_p
        triN = cst_bf[:, 2, :] if samp else cst_bf[:, 0, :]
        UN = cst_bf[:, 3, :] if samp else cst_bf[:, 1, :]
        sf = fence_l if samp else []
        bc = pbank()

        def fn(e, bc=bc):
            ins = None
            for h in range(4):
                ins = e.matmul(banks[bc][:, h * 128:(h + 1) * 128], lhsT=sp_hi[:, h * 128:(h + 1) * 128], rhs=triN,
                               start=True, stop=False)
                ins = e.matmul(banks[bc][:, h * 128:(h + 1) * 128], lhsT=sp_lo[:, h * 128:(h + 1) * 128], rhs=triN,
                               start=False, stop=True)
            return ins
        S.op("pe", fn, reads=["sp_hi", "sp_lo", "cst_bf0", "cst_bf1"], writes=[R(bc)])
        S.op("act", lambda e, bc=bc: e.activation(out=ecT.rearrange("p a b -> p (a b)"), in_=banks[bc][:, 0:512], func=AF.Exp),
             reads=[R(bc)], writes=["ecT"])
        S.op("act", lambda e, bc=bc: e.activation(out=encT.rearrange("p a b -> p (a b)"), in_=banks[bc][:, 0:512], func=AF.Exp,
                                                 scale=-1.0), reads=[R(bc)], writes=["encT"])
        pfree(bc)
        bu = pbank()
        def fn(e, bu=bu):
            e.matmul(banks[bu][:, 0:512], lhsT=UN, rhs=sp_hi[:, :], start=True, stop=False)
            return e.matmul(banks[bu][:, 0:512], lhsT=UN, rhs=sp_lo[:, :], start=False, stop=True)
        S.op("pe", fn, reads=["sp_hi", "sp_lo", "cst_bf0", "cst_bf1"], writes=[R(bu)])
        S.op("act", lambda e, bu=bu: e.activation(out=eU[:, :], in_=banks[bu][:, 0:512], func=AF.Exp), reads=[R(bu)], writes=["eU"])
        pfree(bu)
        yield
        bt = pbank()
        pv = banks_bf[bt]

        def fn(e, pv=pv):
            ins = None
            for j in range(8):
                ins = e.transpose(out=pv[:, j * 128:(j + 1) * 128], in_=qk[par][:, j * 128:(j + 1) * 128], identity=ident_bf[:, :])
            return ins
        S.op("pe", fn, reads=[("qk", par, 0), ("qk", par, 1), "ident_bf"], writes=[R(bt)])
        S.op("dve", lambda e, pv=pv: e.tensor_tensor(out=qTt.rearrange("p a b -> p (a b)"), in0=pv[:, 0:512],
                                                     in1=ecT.rearrange("p a b -> p (a b)"), op=ALU.mult),
             reads=[R(bt), "ecT"], writes=["qTt"])
        S.op("dve", lambda e, pv=pv: e.tensor_tensor(out=kTt.rearrange("p a b -> p (a b)"), in0=pv[:, 512:1024],
                                                     in1=encT.rearrange("p a b -> p (a b)"), op=ALU.mult),
             reads=[R(bt), "encT"], writes=["kTt"])
        pfree(bt)
        S.op("dve", lambda e: e.tensor_tensor(out=kdec[:, :], in0=qk[par][:, 512:1024], in1=eU[:, :], op=ALU.mult),
             reads=[("qk", par, 1), "eU"], writes=["kdec"])
        yield
        bs_ = pbank()

        def fn(e, bs_=bs_):
            ins = None
            for h in range(4):
                ins = e.matmul(banks[bs_][:, h * 128:(h + 1) * 128], lhsT=kTt[:, h, :], rhs=qTt[:, h, :], start=True, stop=True)
            return ins
        S.op("pe", fn, reads=["kTt", "qTt"], writes=[R(bs_)])
        S.op("dve", lambda e, bs_=bs_: e.tensor_tensor(
            out=scT[:, :, :], in0=banks[bs_][:, 0:512].rearrange("p (a b) -> p a b", a=4),
            in1=msk.unsqueeze(1).to_broadcast([128, 4, 128]), op=ALU.mult),
            reads=[R(bs_), "cst"], writes=["scT"])
        pfree(bs_)
        yield
        if samp:
            ob = [pbank(), pbank()]
            gla_C1.ob = ob

            def fn(e):
                ins = None
                for h in range(4):
                    o_ap = banks[ob[h // 2]][:, (h % 2) * 256:(h % 2) * 256 + 256]
                    ins = e.matmul(o_ap, lhsT=scT[:, h, :], rhs=vbf_s[:, hsl(h)], start=(h % 2 == 0), stop=False, skip_group_check=True)
                return ins
            S.op("pe", fn, reads=["scT", ("vbf_s", 0), ("vbf_s", 1)] + sf, writes=[R(ob[0]), R(ob[1])])
            S.op("dve", lambda e: e.tensor_tensor(
                out=qm8[:, :, :, :], in0=qTt.unsqueeze(2).to_broadcast([128, 4, 8, 128]),
                in1=mq.unsqueeze(1).to_broadcast([128, 4, 8, 128]), op=ALU.mult),
                reads=["qTt", "mq"] + sf, writes=["qm8"], after=["wsall", "T1", "vecrows", "waug_st0", "waug_st1"])
            S.op("dve", lambda e: e.tensor_tensor(
                out=kdm8[:, :, :], in0=kdec.unsqueeze(1).to_broadcast([128, 8, 512]),
                in1=bmask8.unsqueeze(2).to_broadcast([128, 8, 512]), op=ALU.mult),
                reads=["kdec", "cst"] + sf, writes=["kdm8"])
            S.op("dve", lambda e: e.tensor_copy(out=elast[:, :, :],
                                                in_=ecT.rearrange("p h (b t) -> p h b t", t=8)[:, :, :, 7]),
                 reads=["ecT"], writes=["elast"])
            yield

    def out_stage(ti, ob, sg_t, sg_res, sf, og=og, ogT=ogT, tag=""):
        ssq4, ssq4_r = small(4)
        for h in range(4):
            o_ap = banks[ob[h // 2]][:, (h % 2) * 256:(h % 2) * 256 + 256]
            S.op("act", lambda e, h=h, o_ap=o_ap: e.activation(out=og[:, hsl(h)], in_=o_ap, func=AF.Square,
                                                              accum_out=ssq4[:, h:h + 1]),
                 reads=[R(ob[h // 2])], writes=[("og" + tag, h), (ssq4_r, h)])
        rs4, rs4_r = rstd_ops(ssq4, [(ssq4_r, h) for h in range(4)], DV, 4)
        for h in range(4):
            o_ap = banks[ob[h // 2]][:, (h % 2) * 256:(h % 2) * 256 + 256]
            S.op("dve", lambda e, h=h, o_ap=o_ap: e.scalar_tensor_tensor(
                out=og[:, hsl(h)], in0=o_ap, scalar=rs4[:, h:h + 1], in1=sg_t[:, hsl(h)], op0=ALU.mult, op1=ALU.mult),
                reads=[R(ob[h // 2]), rs4_r, (sg_res, h // 2)] + sf, writes=[("og" + tag, h)])
        pfree(*ob)
        yield
        transposes8(og, [("og" + tag, h) for h in range(4)], ogT, "ogT" + tag, "dve")
        yield
        for j in range(2):
            b = proj_chunk(ogT, "ogT" + tag, w_out, "wA_out", j * 512, 512)
            sl = slice(j * 512, (j + 1) * 512)
            S.op("dve", lambda e, b=b, sl=sl: e.tensor_tensor(out=x1all[:, ti, sl], in0=x1all[:, ti, sl], in1=banks[b][:, 0:512],
                                                              op=ALU.add), reads=[R(b), ("x1", ti, j)], writes=[("x1", ti, j)])
            pfree(b)
            yield

    def gla_C2(ti, first_prompt, last):
        par = ti % 2
        vres = [(("vbf", par), 0), (("vbf", par), 1)]
        v_t, sg_t, sg_r = vbf[par], sg[par], ("sg", par)
        if ti == NT - 1 and PAIR_LAST:
            vres = [("vbf3", 0), ("vbf3", 1)]
            v_t, sg_t, sg_r = og_s, ogT_s.rearrange("p a b -> p (a b)"), "sg3"
        ob = [pbank(), pbank()]

        def fn(e):
            ins = None
            for h in range(4):
                o_ap = banks[ob[h // 2]][:, (h % 2) * 256:(h % 2) * 256 + 256]
                ins = e.matmul(o_ap, lhsT=scT[:, h, :], rhs=v_t[:, hsl(h)],
                               start=(h % 2 == 0), stop=first_prompt, skip_group_check=True)
                if not first_prompt:
                    ins = e.matmul(o_ap, lhsT=qTt[:, h, :], rhs=Sbf[:, h, :], start=False, stop=True, skip_group_check=True)
            return ins
        S.op("pe", fn, reads=["scT", "qTt", "Sbf"] + vres, writes=[R(ob[0]), R(ob[1])])
        sb_ = [pbank(), pbank()]

        def fn(e):
            ins = None
            for h in range(4):
                ins = e.matmul(banks[sb_[h // 2]][:, (h % 2) * 256:(h % 2) * 256 + 256], lhsT=kdec[:, h * 128:(h + 1) * 128],
                               rhs=v_t[:, hsl(h)], start=(h % 2 == 0), stop=True, skip_group_check=True)
            return ins
        S.op("pe", fn, reads=["kdec"] + vres, writes=[R(sb_[0]), R(sb_[1])])
        for h in range(4):
            s_ps = banks[sb_[h // 2]][:, (h % 2) * 256:(h % 2) * 256 + 256]
            if first_prompt:
                S.op("dve", lambda e, h=h, s_ps=s_ps: e.tensor_copy(out=Sst[:, h, :], in_=s_ps),
                     reads=[R(sb_[h // 2])], writes=[("Sst", h)])
            else:
                S.op("dve", lambda e, h=h, s_ps=s_ps: e.scalar_tensor_tensor(
                    out=Sst[:, h, :], in0=Sst[:, h, :], scalar=ecT[:, h, 127:128], in1=s_ps, op0=ALU.mult, op1=ALU.add),
                    reads=[R(sb_[h // 2]), "ecT", ("Sst", h)], writes=[("Sst", h)])
        pfree(*sb_)
        if last:
            S.op("sp", lambda e: e.dma_start(out=stp_d.rearrange("h d v -> d h v"), in_=Sst[:, :, :]),
                 reads=[("Sst", h) for h in range(4)], dma="stp")
        else:
            S.op("dve", lambda e: e.tensor_copy(out=Sbf.rearrange("p a b -> p (a b)"), in_=Sst.rearrange("p a b -> p (a b)")),
                 reads=[("Sst", h) for h in range(4)], writes=["Sbf"])
        for _ in out_stage(ti, ob, sg_t, sg_r, []):
            yield

    def sample_bg():
        sf = fence_l
        ob = gla_C1.ob
        vres = [("vbf_s", 0), ("vbf_s", 1)]
        stage_res = [("wstA", i) for i in range(8)]

        LD.flush()

        def load_S(bq):
            i = bq % NSB
            S.op("sp", lambda e: e.dma_start(out=Sb[i][:, :, :], in_=st_d[bq].rearrange("h d v -> d h v")),
                 reads=sf, writes=[("Sb", i)], after=stage_res, dma="sbin%d" % i)
        def do_cast(bq):
            i, i2 = bq % NSB, bq % 2
            cast_op(Sb_bf[i2].rearrange("p a b -> p (a b)"), Sb[i].rearrange("p a b -> p (a b)"), None,
                    reads=[("Sb", i)] + sf, writes=[("Sb_bf", i2)])

        for bq in range(min(NSB - 1, 16)):
            load_S(bq)
        do_cast(0)
        for bq in range(16):
            i = bq % NSB
            i2 = bq % 2
            jj, bb = bq // 8, bq % 8
            if bq + NSB - 1 < 16:
                load_S(bq + NSB - 1)
            sb_ = [pbank(), pbank()]

            def fn(e, jj=jj, bb=bb, sb_=sb_):
                ins = None
                for h in range(4):
                    ins = e.matmul(banks[sb_[h // 2]][:, (h % 2) * 256:(h % 2) * 256 + 256],
                                   lhsT=kdm8[64 * jj:64 * jj + 64, bb, h * 128:(h + 1) * 128],
                                   rhs=vbf_s[64 * jj:64 * jj + 64, hsl(h)],
                                   start=(h % 2 == 0), stop=True, skip_group_check=True)
                return ins
            S.op("pe", fn, reads=["kdm8"] + vres + sf, writes=[R(sb_[0]), R(sb_[1])])
            if bq + 1 < 16:
                do_cast(bq + 1)
            bq_last = (bq == 15)

            def fn(e, i2=i2, jj=jj, bb=bb, bq_last=bq_last):
                ins = None
                for h in range(4):
                    o_ap = banks[ob[h // 2]][64 * jj:64 * jj + 64, (h % 2) * 256:(h % 2) * 256 + 256]
                    ins = e.matmul(o_ap, lhsT=qm8[:, h, bb, 64 * jj:64 * jj + 64], rhs=Sb_bf[i2][:, h, :],
                                   start=False, stop=(bq_last and h == 3), skip_group_check=True)
                return ins
            S.op("pe", fn, reads=["qm8", ("Sb_bf", i2)] + sf, writes=[R(ob[0]), R(ob[1])])
            for h in range(4):
                s_ps = banks[sb_[h // 2]][:, (h % 2) * 256:(h % 2) * 256 + 256]
                S.op("dve", lambda e, h=h, s_ps=s_ps, i=i, bq=bq: e.scalar_tensor_tensor(
                    out=Sb[i][:, h, :], in0=Sb[i][:, h, :], scalar=elast[:, h, bq:bq + 1], in1=s_ps,
                    op0=ALU.mult, op1=ALU.add),
                    reads=[R(sb_[h // 2]), "elast", ("Sb", i)] + sf, writes=[("Sb", i)])
            pfree(*sb_)
            S.op("pool", lambda e, bq=bq, i=i: e.dma_start(out=sts_d[bq].rearrange("h d v -> d h v"), in_=Sb[i][:, :, :]),
                 reads=[("Sb", i)] + sf, dma="sbout%d" % i)
            yield
        for _ in out_stage(0, ob, sg_s, "sg_s", sf, og=og_s, ogT=ogT_s, tag="_s"):
            yield

    def run_steps(step_fn, nsteps, bgs=()):
        live_bg = []
        for s in range(nsteps):
            for bgd in bgs:
                if bgd.get("gen") is not None and bgd.get("drain_at") is not None and s >= bgd["drain_at"]:
                    for _ in bgd["gen"]:
                        pass
                    bgd["gen"] = None
            gens = step_fn(s)
            for bgd in bgs:
                if s == bgd["start"]:
                    bgd["gen"] = bgd["fn"]()
            live = list(gens)
            while live:
                nxt = []
                for g in live:
                    try:
                        next(g)
                        nxt.append(g)
                    except StopIteration:
                        pass
                live = nxt
                for bgd in bgs:
                    if bgd.get("gen") is not None:
                        try:
                            next(bgd["gen"])
                        except StopIteration:
                            bgd["gen"] = None
        for bgd in bgs:
            if bgd.get("gen") is not None and not bgd.get("keep"):
                for _ in bgd["gen"]:
                    pass
                bgd["gen"] = None

    if NT > 1:
        S.op("pool", lambda e: e.memset(Sst[:, :, :], 0.0), writes=[("Sst", h) for h in range(4)])

    def stepA(s):
        gens = []
        if s + 2 < NT:
            load_x(s + 2)
        t2 = s - 2
        if 1 <= t2 < NT:
            gens.append(gla_C2(t2, first_prompt=(t2 == 1), last=(t2 == NT - 1)))
        t1 = s - 1
        if 0 <= t1 < NT:
            gens.append(gla_C1(t1))
        if PAIR_LAST and s == NT - 2:
            gens.append(_double_first(gla_P(s)))
            gens.append(_delayed(gla_P(NT - 1, fronts=False, c1front=False), 2))
        elif PAIR_LAST and s == NT - 1:
            gens.append(_one(lambda: gla_C1_front(NT - 1)))
        elif s < NT:
            gens.append(_double_first(gla_P(s)) if s >= 2 else gla_P(s))
        return gens

    def _delayed(g, n):
        for _ in range(n):
            yield
        for _ in g:
            yield

    def _one(fn):
        fn()
        yield

    def swap_bg():
        wst_ctr[0] = 0
        while LDB.pos < len(LDB.items):
            for _ in range(1 if PRECAST else 2):
                if LDB.pos < len(LDB.items) and LDB.items[LDB.pos][0][1] // 512 <= swap_state["allowed"]:
                    LDB.step()
            yield

    gla_front_a(0)
    gla_front_b(0)
    bg_s = {"start": 2, "fn": sample_bg, "drain_at": DRAIN_AT if DRAIN_AT is not None else ALIAS0 - 2}
    bg_w = {"start": NT - 2 if PAIR_LAST else NT - 1, "fn": swap_bg, "drain_at": None, "keep": True}
    def mlp_front_a(ti):
        x1 = x1all[:, ti, :]
        x1r = [("x1", ti, 0), ("x1", ti, 1)]
        ssq, ssq_r = small()
        S.op("act", lambda e: e.activation(out=hB[:, :], in_=x1, func=AF.Square, accum_out=ssq),
             reads=x1r, writes=["hB", ssq_r])
        rs, rs_r = rstd_ops_pool(ssq, [ssq_r], D, 1)
        S.op("act", lambda e: e.activation(out=hB[:, :], in_=x1, func=AF.Identity, scale=rs),
             reads=x1r + [rs_r], writes=["hB"])

    def mlp_front_b(ti):
        transposes8(hB, ["hB"], hTB2[ti % 2], ("hTB", ti % 2), "dve")

    def mlp_front_early():
        dead = ["hA", ("hTA", 0), ("hTA", 1)]
        S.pending["act"] = list(dead)
        S.pending["dve"] = list(dead)
        mlp_front_a(0)
        mlp_front_b(0)
        S.pending.pop("act", None)
        S.pending.pop("dve", None)
        yield

    EARLY_FRONT = NT >= 6
    bgs_a = [bg_s, bg_w]
    if EARLY_FRONT:
        bgs_a.append({"start": NT + 1, "fn": mlp_front_early, "drain_at": None})
    if PRECAST:
        bgs_a.append({"start": 5, "fn": precast_bg, "drain_at": NT - 3})
    run_steps(stepA, NT + 2, bgs=bgs_a)

    LD.flush()
    S.barrier(lambda e: e.memset(smalls[:, 61:62], 0.0))
    bg_w["gen"] = None
    S.op("sp", lambda e: e.dma_start(out=lng_bc[:, :], in_=clng_d.partition_broadcast(128)), writes=["lng"], dma="c6")
    S.op("sp", lambda e: e.dma_start(out=lnb_bc[:, :], in_=clnb_d.partition_broadcast(128)), writes=["lnb"], dma="c7")
    S.op("sp", lambda e: e.dma_start(out=gfin_bc[:, :], in_=nf_d.partition_broadcast(128)), writes=["gfin"], dma="c8")
    for cb in range(2):
        if PRECAST:
            S.op("sp", lambda e, cb=cb: e.dma_start(out=w_out[:, :, cb * 512:(cb + 1) * 512], in_=wscr2[:, :, cb * 512:(cb + 1) * 512]),
                 reads=[("wscr", 6 + cb, kt) for kt in range(8)], writes=[("wB_out", cb * 512, kt) for kt in range(8)],
                 dma="swo%d" % cb)
        else:
            for kp in range(8):
                LDB.add(("wB_out", cb * 512), lambda kp=kp, cb=cb: load_weight_chunk(
                    cwout_d, kp, cb * 512, 512, w_out, None, [], "wB_out", wstageB, "wstB", []))
    loaders["wB_in"] = LDB
    loaders["wB_out"] = LDB

    def mlp_ln_chain(ti):
        par = ti % 2
        samp = (ti == 0)
        vtp = vt[par]
        mv, mv_r = small(2)
        S.op("dve", lambda e: e.bn_aggr(out=mv, in_=bnst[par][:, 0:12]), reads=[("bnst", par, 0), ("bnst", par, 1)], writes=[mv_r])
        nmr, nmr_r = small()
        rsv, rsv_r = rstd_ops_pool(mv[:, 1:2], [mv_r], 1.0, 1)
        S.op("dve", lambda e: e.tensor_scalar(out=nmr, in0=mv[:, 0:1], scalar1=rsv, scalar2=-1.0, op0=ALU.mult, op1=ALU.mult),
             reads=[mv_r, rsv_r], writes=[nmr_r])
        for j in range(2):
            sl = slice(j * 512, (j + 1) * 512)
            S.op("act", lambda e, sl=sl: e.activation(out=vtp[:, sl], in_=vtp[:, sl], func=AF.Identity, scale=rsv, bias=nmr),
                 reads=[("vt", par, j), rsv_r, nmr_r], writes=[("vt", par, j)])
        for j in range(2):
            sl = slice(j * 512, (j + 1) * 512)
            S.op("dve", lambda e, sl=sl: e.tensor_tensor(out=vtp[:, sl], in0=vtp[:, sl], in1=lng_bc[:, sl], op=ALU.mult),
                 reads=[("vt", par, j), "lng"], writes=[("vt", par, j)])
            if samp:
                S.op("dve", lambda e, sl=sl: e.tensor_tensor(out=vtp[:, sl], in0=vtp[:, sl], in1=lnb_bc[:, sl], op=ALU.add),
                     reads=[("vt", par, j), "lnb"], writes=[("vt", par, j)])
                S.op("act", lambda e, sl=sl: e.activation(out=vn[:, sl], in_=vtp[:, sl], func=AF.Copy),
                     reads=[("vt", par, j)], writes=[("vn", j)])
            else:
                S.op("dve", lambda e, sl=sl: e.tensor_tensor(out=vn[:, sl], in0=vtp[:, sl], in1=lnb_bc[:, sl], op=ALU.add),
                     reads=[("vt", par, j), "lnb"], writes=[("vn", j)])
        if samp:
            S.op("sp", lambda e: e.dma_start(out=vs_d, in_=vtp[:, :]), reads=[("vt", par, 0), ("vt", par, 1)], dma="vs")


    def mlp_P(ti):
        par = ti % 2
        hTB = hTB2[par]
        hTB_r = ("hTB", par)
        first = (ti == 0)
        if first:
            for j in range(2):
                b = proj_chunk(hTB, hTB_r, w_in_B, "wB_in", j * 512, 512, 128)
                S.op("dve", lambda e, b=b, j=j: e.tensor_copy(out=ubuf[par][:, j * 512:(j + 1) * 512], in_=banks[b][:, 0:512]),
                     reads=[R(b)], writes=[("us", par, j)])
                pfree(b)
                yield
        for j in range(2):
            b = proj_chunk(hTB, hTB_r, w_in_B, "wB_in", D + j * 512, 512, 128)
            sl = slice(j * 512, (j + 1) * 512)
            S.op("act", lambda e, b=b, sl=sl: e.activation(out=vt[par][:, sl], in_=banks[b][:, 0:512], func=AF.Copy),
                 reads=[R(b)], writes=[("vt", par, j)])
            S.op("dve", lambda e, sl=sl, j=j: e.bn_stats(out=bnst[par][:, 6 * j:6 * j + 6], in_=vt[par][:, sl]),
                 reads=[("vt", par, j)], writes=[("bnst", par, j)])
            pfree(b)
            if j == 0 and ti + 1 < NT:
                mlp_front_a(ti + 1)
            if j == 1:
                mlp_ln_chain(ti)
            yield
        if ti + 1 < NT:
            mlp_front_b(ti + 1)
            yield
        gb = [proj_chunk(hTB, hTB_r, w_in_B, "wB_in", 2 * D + j * 512, 512, 128) for j in range(2)]
        for j in range(2):
            S.op("act", lambda e, b=gb[j], j=j: e.activation(out=sg1[par][:, j * 512:(j + 1) * 512], in_=banks[b][:, 0:512],
                                                            func=AF.Silu), reads=[R(gb[j])], writes=[("sg1", par, j)])
        pfree(*gb)
        yield
        for j in range(2):
            sl = slice(j * 512, (j + 1) * 512)
            if first:
                S.op("dve", lambda e, sl=sl: e.tensor_tensor(out=ubuf[par][:, sl], in0=ubuf[par][:, sl], in1=sg1[par][:, sl],
                                                             op=ALU.mult),
                     reads=[("us", par, j), ("sg1", par, j)], writes=[("us", par, j)])
            else:
                b = proj_chunk(hTB, hTB_r, w_in_B, "wB_in", j * 512, 512, 128)
                S.op("dve", lambda e, b=b, sl=sl: e.tensor_tensor(out=ubuf[par][:, sl], in0=banks[b][:, 0:512],
                                                                  in1=sg1[par][:, sl], op=ALU.mult),
                     reads=[R(b), ("sg1", par, j)], writes=[("us", par, j)])
                pfree(b)
            yield
        if first:
            LDB.flush()

    def mlp_C1(ti):
        par = ti % 2
        samp = (ti == 0)
        wsT = wsT_s if samp else wsT_p
        bsT = bsT_s if samp else bsT_p
        mb = [pbank(), pbank()]

        def fn(e):
            ins = None
            for g in range(4):
                ins = e.matmul(banks[mb[g // 2]][:, (g % 2) * 256:(g % 2) * 256 + 256], lhsT=wsT[:, g, :],
                               rhs=vn[:, hsl(g)], start=(g % 2 == 0), stop=True, skip_group_check=True)
            return ins
        S.op("pe", fn, reads=[("vn", 0), ("vn", 1)] + [("wsT_s" if samp else "wsT_p", g) for g in range(4)],
             writes=[R(mb[0]), R(mb[1])])
        for g in range(4):
            m_ap = banks[mb[g // 2]][:, (g % 2) * 256:(g % 2) * 256 + 256]
            S.op("dve", lambda e, g=g, m_ap=m_ap: e.scalar_tensor_tensor(
                out=zb[:, hsl(g)], in0=m_ap, scalar=bsT[:, g:g + 1], in1=ubuf[par][:, hsl(g)],
                op0=ALU.add, op1=ALU.mult),
                reads=[R(mb[g // 2]), "bsT_s", "vecT", ("us", par, g // 2)], writes=[("zb", g)])
        pfree(*mb)
        yield

    def mlp_C2(ti):
        transposes8(zb, [("zb", g) for g in range(4)], zT, "zT", "act")
        yield
        ssq2, ssq2_r = small(2)
        yo = yout[0]
        for j in range(2):
            b = proj_chunk(zT, "zT", w_out, "wB_out", j * 512, 512, 128)
            sl = slice(j * 512, (j + 1) * 512)
            S.op("dve", lambda e, b=b, sl=sl: e.tensor_tensor(out=x1all[:, ti, sl], in0=x1all[:, ti, sl], in1=banks[b][:, 0:512],
                                                              op=ALU.add), reads=[R(b), ("x1", ti, j)], writes=[("x1", ti, j)])
            S.op("act", lambda e, sl=sl, j=j: e.activation(out=yo[:, sl], in_=x1all[:, ti, sl], func=AF.Square,
                                                           accum_out=ssq2[:, j:j + 1]),
                 reads=[("x1", ti, j)], writes=[("yo", 0, j), (ssq2_r, j)])
            pfree(b)
            yield
        sst, sst_r = small()
        S.op("dve", lambda e: e.tensor_tensor(out=sst, in0=ssq2[:, 0:1], in1=ssq2[:, 1:2], op=ALU.add),
             reads=[(ssq2_r, 0), (ssq2_r, 1)], writes=[sst_r])
        rs, rs_r = rstd_ops_pool(sst, [sst_r], D, 1)
        for j in range(2):
            sl = slice(j * 512, (j + 1) * 512)
            S.op("dve", lambda e, sl=sl: e.scalar_tensor_tensor(out=yo[:, sl], in0=x1all[:, ti, sl], scalar=rs, in1=gfin_bc[:, sl],
                                                                op0=ALU.mult, op1=ALU.mult),
                 reads=[("x1", ti, j), rs_r, "gfin"], writes=[("yo", 0, j)])
        S.op("sp", lambda e: e.dma_start(out=y_d[ti * 128:(ti + 1) * 128, :], in_=yo[:, :]),
             reads=[("yo", 0, 0), ("yo", 0, 1)], dma="yout0")
        yield

    def stepB(s):
        gens = []
        t2 = s - 2
        if 0 <= t2 < NT:
            gens.append(mlp_C2(t2))
        t1 = s - 1
        if 0 <= t1 < NT:
            gens.append(mlp_C1(t1))
        if s < NT:
            gens.append(mlp_P(s))
        return gens

    if not EARLY_FRONT:
        mlp_front_a(0)
        mlp_front_b(0)
    run_steps(stepB, NT + 2)

    out_keys = [k for k in S.cnt if k.startswith("dma:yout") or k.startswith("dma:sbout") or k in ("dma:vs", "dma:stp")]
    S.final_wait("sp", out_keys)

    sem_keys = list(S.cnt.keys())
    sems = {k: es.enter_context(nc.semaphore("s_" + k.replace(":", "_"))) for k in sem_keys}
    block = es.enter_context(nc.Block())

    def emit(eng_obj, name):
        for waits, fn, sig in S.prog[name]:
            for k, v in waits:
                eng_obj.wait_ge(sems[k], v)
            if fn is not None:
                ins = fn(eng_obj)
                ins.then_inc(sems[sig[0]], sig[1])

    @block.sync
    def _(e):
        emit(e, "sp")

    @block.tensor
    def _(e):
        emit(e, "pe")

    @block.scalar
    def _(e):
        emit(e, "act")

    @block.vector
    def _(e):
        emit(e, "dve")

    @block.gpsimd
    def _(e):
        emit(e, "pool")

    es.close()
    return nc


def _consts():
    c = np.zeros((128, 1032), np.float32)
    s = np.arange(128)[:, None]
    t = np.arange(128)[None, :]
    c[:, 0:128] = np.eye(128, dtype=np.float32)
    c[:, 128:256] = np.where(s <= t, -1.0 / 16, 0.0)
    c[:, 256:384] = np.where(s > t, -1.0 / 16, 0.0)
    c[:, 384:512] = np.where(s <= t, 1.0, 0.0)
    same = (s // 8) == (t // 8)
    c[:, 512:640] = np.where(same & (s <= t), -1.0 / 16, 0.0)
    c[:, 640:768] = np.where(same & (s > t), -1.0 / 16, 0.0)
    c[:, 768:896] = np.where(same & (s <= t), 1.0, 0.0)
    bb = np.arange(8)[None, :]
    c[:, 896:904] = np.where(((s // 8) % 8) == bb, 1.0, 0.0)
    r = np.arange(8)[:, None]
    c[0:8, 904:1032] = np.where((t % 8) == r, 1.0, 0.0)
    mq = np.where(((t // 8) % 8) == np.arange(8)[:, None], 1.0, 0.0).astype(np.float32).reshape(1024)
    return c, mq


_NC_CACHE = {}


def kernel(x_prompt, x_sample, state_gla, norm_g, gla_w_in, gla_w_a_up, gla_b_a_up, gla_g_onorm,
           gla_w_out, cmlp_w_in, cmlp_ln_g, cmlp_ln_b, cmlp_w_spatial, cmlp_b_spatial, cmlp_w_out,
           norm_final):
    f = lambda a: np.ascontiguousarray(np.asarray(a, dtype=np.float32))
    x_prompt, x_sample, state_gla = f(x_prompt), f(x_sample), f(state_gla)
    Bp, T, _ = x_prompt.shape
    NTP = T // 128
    NT = NTP + 1
    if NTP not in _NC_CACHE:
        _NC_CACHE[NTP] = build(NTP)
    nc = _NC_CACHE[NTP]
    consts, mq = _consts()
    vecrows = np.concatenate([f(norm_g).reshape(16, 128), f(gla_g_onorm).reshape(2, 128),
                              f(cmlp_b_spatial).reshape(4, 128)], axis=0)
    shared = {
        "gla_w_in": f(gla_w_in)[0], "gla_w_a_up": f(gla_w_a_up)[0], "gla_b_a_up": f(gla_b_a_up).reshape(1, 512),
        "gla_w_out": f(gla_w_out)[0], "cmlp_w_in": f(cmlp_w_in)[0], "cmlp_ln_g": f(cmlp_ln_g).reshape(D),
        "cmlp_ln_b": f(cmlp_ln_b).reshape(D), "cmlp_w_spatial": f(cmlp_w_spatial)[0],
        "cmlp_w_out": f(cmlp_w_out)[0], "norm_final": f(norm_final).reshape(D),
        "vecrows": np.ascontiguousarray(vecrows), "consts": consts, "mq": mq,
    }
    in_maps = []
    for c in range(NCORES):
        xs = x_sample[16 * c:16 * (c + 1)].reshape(128, D)
        xc = np.concatenate([xs, x_prompt[c].reshape(NTP * 128, D)], axis=0)
        m = dict(shared)
        m["x"] = np.ascontiguousarray(xc)
        m["st"] = np.ascontiguousarray(state_gla[0, 16 * c:16 * (c + 1)])
        in_maps.append(m)
    res = run_bass_kernel_spmd(nc, in_maps, core_ids=list(range(NCORES)))
    outs = res.results
    y_prompt = np.stack([outs[c]["y"][128:].reshape(T, D) for c in range(NCORES)], axis=0)
    y_sample = np.concatenate([outs[c]["y"][:128].reshape(16, 8, D) for c in range(NCORES)], axis=0)
    st_p = np.stack([outs[c]["stp"] for c in range(NCORES)], axis=0)[None]
    st_s = np.concatenate([outs[c]["sts"] for c in range(NCORES)], axis=0)[None]
    v_s = np.concatenate([outs[c]["vs"].reshape(16, 8, D) for c in range(NCORES)], axis=0)[None]
    return (y_prompt.astype(np.float32), y_sample.astype(np.float32), st_p.astype(np.float32),
            st_s.astype(np.float32), v_s.astype(np.float32))
```
